# Optimizing a Trainium2 kernel written in Bass

```python
import jax, jax.numpy as jnp
from jax import lax
import numpy as np

D_MODEL = 2048
BATCH = 4
SEQ = 4096
DEPTH = 2

N_A_LAYERS = DEPTH // 2
N_B_LAYERS = DEPTH - N_A_LAYERS

SSD_EXPAND = 2
SSD_D_INNER = SSD_EXPAND * D_MODEL
SSD_HEAD_DIM = 64
SSD_N_HEADS = SSD_D_INNER // SSD_HEAD_DIM
SSD_N_GROUPS = 8
SSD_HEADS_PER_GROUP = SSD_N_HEADS // SSD_N_GROUPS
SSD_D_STATE = 128
SSD_CONV_W = 4
SSD_CHUNK = 256
SSD_BC_DIM = SSD_N_GROUPS * SSD_D_STATE
SSD_CONV_DIM = SSD_D_INNER + 2 * SSD_BC_DIM
SSD_IN_DIM = SSD_D_INNER + SSD_CONV_DIM + SSD_N_HEADS

DIL_PATTERNS = ((128, 1), (512, 4), (2048, 16))
DIL_N_GROUPS = len(DIL_PATTERNS)
DIL_HEADS = 8
DIL_HEAD_DIM = 128
DIL_Q_WIDTH = DIL_N_GROUPS * DIL_HEADS * DIL_HEAD_DIM
DIL_OUT_WIDTH = DIL_HEADS * DIL_HEAD_DIM
DIL_IN_DIM = DIL_Q_WIDTH + DIL_OUT_WIDTH
DIL_KV_DIM = 2 * DIL_Q_WIDTH
DIL_BLOCK = 128

DEEPNORM_ALPHA = (2 * DEPTH) ** 0.25
DEEPNORM_BETA = (8 * DEPTH) ** -0.25
LN_EPS = 1e-5
RMS_EPS = 1e-5

kernel_name = "hybrid_yoco_ssd_dilated_alibi_deepnorm"


def _layer_norm(x, g, b):
    xf = x.astype(jnp.float32)
    mu = jnp.mean(xf, -1, keepdims=True)
    var = jnp.mean(jnp.square(xf - mu), -1, keepdims=True)
    return ((xf - mu) * lax.rsqrt(var + LN_EPS)).astype(x.dtype) * g + b


def _adaln(c, w, b):
    mod = jax.nn.silu(c) @ w + b
    shift, scale, gate = jnp.split(mod, 3, axis=-1)
    return shift[:, None, :], scale[:, None, :], gate[:, None, :]


def _causal_depthwise_conv(x, w, b):
    y = lax.conv_general_dilated(
        x, w[:, None, :], window_strides=(1,), padding=[(SSD_CONV_W - 1, 0)],
        dimension_numbers=("NWC", "WIO", "NWC"), feature_group_count=x.shape[-1])
    return y + b


def _ssd_chunked(xdt, dtA, Bm, Cm):
    f32 = jnp.float32
    bsz, L = xdt.shape[:2]
    G, K, P, N = SSD_N_GROUPS, SSD_HEADS_PER_GROUP, SSD_HEAD_DIM, SSD_D_STATE
    Lp = -(-L // SSD_CHUNK) * SSD_CHUNK
    nc = Lp // SSD_CHUNK

    def chunks(a):
        a = jnp.pad(a, [(0, 0), (0, Lp - L)] + [(0, 0)] * (a.ndim - 2))
        a = a.reshape((bsz, nc, SSD_CHUNK) + a.shape[2:])
        return jnp.moveaxis(a, 1, 0)

    xs = chunks(xdt.reshape(bsz, L, G, K, P).astype(f32))
    As = chunks(dtA.reshape(bsz, L, G, K).astype(f32))
    Bs = chunks(Bm.astype(f32))
    Cs = chunks(Cm.astype(f32))
    causal = jnp.tril(jnp.ones((SSD_CHUNK, SSD_CHUNK), bool))[None, :, :, None, None]

    def step(state, inp):
        xc, ac, bc, cc = inp
        acum = jnp.cumsum(ac, axis=1)
        seg = acum[:, :, None] - acum[:, None, :]
        decay = jnp.exp(jnp.where(causal, seg, -jnp.inf))
        cb = jnp.einsum("blgn,bsgn->blsg", cc, bc)
        y_diag = jnp.einsum("blsgk,bsgkp->blgkp", cb[..., None] * decay, xc)
        y_off = jnp.einsum("blgn,bgkpn->blgkp", cc, state) * jnp.exp(acum)[..., None]
        tail = jnp.exp(acum[:, -1:] - acum)
        new_state = (state * jnp.exp(acum[:, -1])[..., None, None]
                     + jnp.einsum("bsgn,bsgkp->bgkpn", bc, xc * tail[..., None]))
        return new_state, y_diag + y_off

    state0 = jnp.zeros((bsz, G, K, P, N), f32)
    _, ys = lax.scan(step, state0, (xs, As, Bs, Cs))
    ys = jnp.moveaxis(ys, 0, 1).reshape(bsz, Lp, SSD_N_HEADS * P)
    return ys[:, :L]


def _ssd_mixer(h, in_w, conv_w, conv_b, dt_bias, A_log, D_skip, norm_g, out_w):
    bsz, L, _ = h.shape
    proj = h @ in_w
    z, xBC, dt = jnp.split(proj, [SSD_D_INNER, SSD_D_INNER + SSD_CONV_DIM], axis=-1)
    xBC = jax.nn.silu(_causal_depthwise_conv(xBC, conv_w, conv_b))
    xs, Bm, Cm = jnp.split(xBC, [SSD_D_INNER, SSD_D_INNER + SSD_BC_DIM], axis=-1)
    xs = xs.reshape(bsz, L, SSD_N_HEADS, SSD_HEAD_DIM)
    Bm = Bm.reshape(bsz, L, SSD_N_GROUPS, SSD_D_STATE)
    Cm = Cm.reshape(bsz, L, SSD_N_GROUPS, SSD_D_STATE)
    dt = jax.nn.softplus((dt + dt_bias).astype(jnp.float32))
    A = -jnp.exp(A_log.astype(jnp.float32))
    y = _ssd_chunked(xs * dt[..., None], dt * A, Bm, Cm)
    y = y + (xs * D_skip[:, None]).reshape(bsz, L, SSD_D_INNER)
    y = y * jax.nn.silu(z.astype(jnp.float32))
    yg = y.reshape(bsz, L, SSD_N_GROUPS, -1)
    yg = yg * lax.rsqrt(jnp.mean(jnp.square(yg), -1, keepdims=True) + RMS_EPS)
    y = yg.reshape(bsz, L, SSD_D_INNER).astype(h.dtype) * norm_g
    return y @ out_w


def _alibi_slopes():
    n = DIL_N_GROUPS * DIL_HEADS
    s = 2.0 ** (-8.0 * np.arange(1, n + 1) / n)
    return jnp.asarray(s.reshape(DIL_N_GROUPS, DIL_HEADS), dtype=jnp.float32)


def _dilated_window_attention(q, k, v, window, dilation, slopes):
    f32 = jnp.float32
    bsz, L, H, E = q.shape
    span = window // dilation
    M = -(-L // (dilation * DIL_BLOCK)) * DIL_BLOCK
    nb = M // DIL_BLOCK
    pad = M * dilation - L

    def to_blocks(a):
        a = jnp.pad(a, [(0, 0), (0, pad), (0, 0), (0, 0)])
        a = a.reshape(bsz, M, dilation, H, E).transpose(0, 2, 1, 3, 4)
        return a.reshape(bsz, dilation, nb, DIL_BLOCK, H, E)

    def with_prev(a):
        prev = jnp.pad(a[:, :, :-1], [(0, 0), (0, 0), (1, 0), (0, 0), (0, 0), (0, 0)])
        return jnp.concatenate([prev, a], axis=3)

    qb = to_blocks(q)
    kb = with_prev(to_blocks(k))
    vb = with_prev(to_blocks(v))
    s = jnp.einsum("brnqhe,brnkhe->brnhqk", qb, kb,
                   preferred_element_type=f32) * (E ** -0.5)
    qi = jnp.arange(DIL_BLOCK)[:, None]
    kj = jnp.arange(2 * DIL_BLOCK)[None, :]
    delta = qi + DIL_BLOCK - kj
    valid = (delta >= 0) & (delta <= span)
    first = (jnp.arange(nb) == 0)[:, None, None]
    valid = valid[None] & ~(first & (kj < DIL_BLOCK)[None])
    alibi = -slopes[:, None, None] * (delta * dilation).astype(f32)[None]
    s = jnp.where(valid[None, None, :, None], s + alibi[None, None, None], -jnp.inf)
    m = jnp.max(s, -1, keepdims=True)
    p = jnp.exp(s - m)
    den = jnp.sum(p, -1)
    o = jnp.einsum("brnhqk,brnkhe->brnqhe", p, vb.astype(f32))
    o = o / jnp.moveaxis(den, 3, 4)[..., None]
    lse = jnp.moveaxis(m[..., 0] + jnp.log(den), 3, 4)

    def from_blocks(a):
        a = a.reshape((bsz, dilation, M) + a.shape[4:])
        a = jnp.moveaxis(a, 1, 2).reshape((bsz, M * dilation) + a.shape[3:])
        return a[:, :L]

    return from_blocks(o), from_blocks(lse)


def _shared_kv(x, kv_w):
    bsz, L, _ = x.shape
    k, v = jnp.split(x @ kv_w, 2, axis=-1)
    shp = (bsz, L, DIL_N_GROUPS, DIL_HEADS, DIL_HEAD_DIM)
    return k.reshape(shp), v.reshape(shp)


def _dilated_mixer(h, k_sh, v_sh, in_w, out_w):
    bsz, L, _ = h.shape
    q, z = jnp.split(h @ in_w, [DIL_Q_WIDTH], axis=-1)
    q = q.reshape(bsz, L, DIL_N_GROUPS, DIL_HEADS, DIL_HEAD_DIM)
    slopes = _alibi_slopes()
    outs, lses = [], []
    for g, (window, dilation) in enumerate(DIL_PATTERNS):
        o, lse = _dilated_window_attention(q[:, :, g], k_sh[:, :, g], v_sh[:, :, g],
                                           window, dilation, slopes[g])
        outs.append(o)
        lses.append(lse)
    o = jnp.stack(outs, 2)
    wts = jax.nn.softmax(jnp.stack(lses, 2), axis=2)
    o = jnp.einsum("blghe,blgh->blhe", o, wts).reshape(bsz, L, DIL_OUT_WIDTH)
    o = o.astype(h.dtype) * jax.nn.silu(z)
    return o @ out_w


def setup_inputs(seed: int = 0) -> dict:
    key = jax.random.key(seed)
    ks = jax.random.split(key, 20)
    f32 = jnp.float32
    nA, nB, D = N_A_LAYERS, N_B_LAYERS, D_MODEL
    nrm = lambda k, shp, sc: jax.random.normal(k, shp, f32) * sc
    dt0 = jnp.exp(jax.random.uniform(ks[8], (nA, SSD_N_HEADS), f32,
                                     np.log(1e-3), np.log(1e-1)))
    return {
        "x": nrm(ks[0], (BATCH, SEQ, D), 1.0),
        "c": nrm(ks[1], (BATCH, D), 1.0),
        "ada_w": nrm(ks[2], (DEPTH, D, 3 * D), 0.1 * D ** -0.5),
        "ada_b": nrm(ks[3], (DEPTH, 3 * D), 0.01),
        "ln_g": 1.0 + nrm(ks[4], (DEPTH, D), 0.01),
        "ln_b": nrm(ks[5], (DEPTH, D), 0.01),
        "a_in_w": nrm(ks[6], (nA, D, SSD_IN_DIM), D ** -0.5),
        "a_conv_w": nrm(ks[7], (nA, SSD_CONV_W, SSD_CONV_DIM), SSD_CONV_W ** -0.5),
        "a_conv_b": nrm(ks[9], (nA, SSD_CONV_DIM), 0.01),
        "a_dt_bias": dt0 + jnp.log(-jnp.expm1(-dt0)),
        "a_A_log": jnp.log(jax.random.uniform(ks[10], (nA, SSD_N_HEADS), f32, 1.0, 16.0)),
        "a_D": 1.0 + nrm(ks[11], (nA, SSD_N_HEADS), 0.01),
        "a_norm_g": 1.0 + nrm(ks[12], (nA, SSD_D_INNER), 0.01),
        "a_out_w": nrm(ks[13], (nA, SSD_D_INNER, D), DEEPNORM_BETA * SSD_D_INNER ** -0.5),
        "kv_w": nrm(ks[14], (D, DIL_KV_DIM), D ** -0.5),
        "b_in_w": nrm(ks[15], (nB, D, DIL_IN_DIM), D ** -0.5),
        "b_out_w": nrm(ks[16], (nB, DIL_OUT_WIDTH, D), DEEPNORM_BETA * DIL_OUT_WIDTH ** -0.5),
    }


def reference(x, c, ada_w, ada_b, ln_g, ln_b, a_in_w, a_conv_w, a_conv_b, a_dt_bias,
              a_A_log, a_D, a_norm_g, a_out_w, kv_w, b_in_w, b_out_w):
    k_sh, v_sh = None, None
    for layer in range(DEPTH):
        shift, scale, gate = _adaln(c, ada_w[layer], ada_b[layer])
        h = x * (1.0 + scale) + shift
        if layer < N_A_LAYERS:
            i = layer
            y = _ssd_mixer(h, a_in_w[i], a_conv_w[i], a_conv_b[i], a_dt_bias[i],
                           a_A_log[i], a_D[i], a_norm_g[i], a_out_w[i])
        else:
            i = layer - N_A_LAYERS
            y = _dilated_mixer(h, k_sh, v_sh, b_in_w[i], b_out_w[i])
        x = _layer_norm(DEEPNORM_ALPHA * x + (1.0 + gate) * y, ln_g[layer], ln_b[layer])
        if layer == N_A_LAYERS - 1:
            k_sh, v_sh = _shared_kv(x, kv_w)
    return x
```

```python
import numpy as np
import concourse.bass as bass
import concourse.mybir as mybir
from concourse.bass_utils import run_bass_kernel_spmd
from contextlib import ExitStack

F32 = mybir.dt.float32
BF16 = mybir.dt.bfloat16
AF = mybir.ActivationFunctionType
ALU = mybir.AluOpType
AX = mybir.AxisListType

ENGINES = ("pe", "act", "dve", "pool", "sp")


class Buf:
    __slots__ = ("name", "last_write", "readers", "dma_n", "sem", "exclusive")

    def __init__(self, name, exclusive=False):
        self.name = name
        self.exclusive = exclusive
        self.last_write = None
        self.readers = []
        self.dma_n = 0
        self.sem = None


class Op:
    __slots__ = ("eng", "fn", "waits", "idx", "is_dma", "dst", "dma_val", "signaled",
                 "semval", "ninc")


class Sched:
    SAME_ENGINE_SYNC = True

    def __init__(self, nc):
        self.nc = nc
        self.q = {e: [] for e in ENGINES}
        self.known = {e: {} for e in ENGINES}
        self.dma_bufs = []
        self.final_bufs = []

    def _event(self, op):
        if op.is_dma:
            return (op.dst, op.dma_val)
        return (op.eng, op.idx)

    def op(self, eng, fn, reads=(), writes=(), dma=False, ninc=1, nowaw=False):
        o = Op()
        o.eng = eng
        o.fn = fn
        o.is_dma = dma
        o.signaled = False
        o.semval = None
        o.ninc = ninc
        o.idx = len(self.q[eng])
        o.dst = None
        o.dma_val = None
        deps = []
        excl = [b for b in reads if b.exclusive and b not in writes]
        if excl:
            assert not dma
            reads = [b for b in reads if not b.exclusive]
            writes = list(writes) + excl
        for b in reads:
            if b.last_write is not None:
                deps.append(b.last_write)
        for b in writes:
            if b.last_write is not None and not nowaw:
                deps.append(b.last_write)
            deps.extend(b.readers)
        if dma:
            assert len(writes) == 1
            o.dst = writes[0]
            if o.dst.dma_n == 0 and o.dst not in self.dma_bufs:
                self.dma_bufs.append(o.dst)
            o.dst.dma_n += ninc
            o.dma_val = o.dst.dma_n
        known = self.known[eng]
        waits = {}
        for d in deps:
            if d is o:
                continue
            key, val = self._event(d)
            if not d.is_dma and d.eng == eng:
                if eng == "pe" or eng == "sp" or not self.SAME_ENGINE_SYNC:
                    continue
            if known.get(key, -1) >= val:
                continue
            if waits.get(key, (-1, None))[0] < val:
                waits[key] = (val, d)
        o.waits = []
        for key, (val, d) in waits.items():
            known[key] = val
            d.signaled = True
            o.waits.append(d)
        for b in reads:
            b.readers.append(o)
        for b in writes:
            b.last_write = o
            if not nowaw:
                b.readers = []
        self.q[eng].append(o)
        return o

    def finish(self, out_bufs):
        self.final_bufs = list(out_bufs)

    def emit(self, stack):
        nc = self.nc
        esem = {}
        for e in ENGINES:
            esem[e] = stack.enter_context(nc.semaphore(f"s_{e}"))
        for i, b in enumerate(self.dma_bufs):
            b.sem = stack.enter_context(nc.semaphore(f"d{i}"))
        for e in ENGINES:
            c = 0
            for o in self.q[e]:
                if not o.is_dma and o.signaled:
                    c += 1
                    o.semval = c
        block = stack.enter_context(nc.Block())
        final_bufs = self.final_bufs

        def run(engine, name):
            for o in self.q[name]:
                for d in o.waits:
                    if d.is_dma:
                        engine.wait_ge(d.dst.sem, 16 * d.dma_val)
                    else:
                        engine.wait_ge(esem[d.eng], d.semval)
                inst = o.fn(engine)
                if o.is_dma:
                    if isinstance(inst, (list, tuple)):
                        assert len(inst) == o.ninc
                        for i_ in inst:
                            i_.then_inc(o.dst.sem, 16)
                    else:
                        assert o.ninc == 1
                        inst.then_inc(o.dst.sem, 16)
                elif o.signaled:
                    inst.then_inc(esem[name], 1)
            if name == "sp":
                for b in final_bufs:
                    engine.wait_ge(b.sem, 16 * b.dma_n)

        @block.sync
        def _(e):
            run(e, "sp")

        @block.scalar
        def _(e):
            run(e, "act")

        @block.vector
        def _(e):
            run(e, "dve")

        @block.gpsimd
        def _(e):
            run(e, "pool")

        @block.tensor
        def _(e):
            run(e, "pe")

    def stats(self):
        return {e: (len(self.q[e]), sum(len(o.waits) for o in self.q[e])) for e in ENGINES}


class Tile:
    __slots__ = ("ap", "buf")

    def __init__(self, ap, buf):
        self.ap = ap
        self.buf = buf

    def __getitem__(self, k):
        return self.ap[k]


class Arena:
    def __init__(self, tensor, nwords, name):
        self.t = tensor
        self.cap = nwords
        self.top = 0
        self.live = []
        self.name = name
        self.peak = 0

    def mark(self):
        return self.top

    def release(self, m):
        self.top = m

    def _mk(self, name, start, words):
        end = start + words
        assert end <= self.cap, f"{self.name} arena overflow: {name} needs {end} > {self.cap}"
        self.peak = max(self.peak, end)
        b = Buf(name)
        keep = []
        for (s, e, ob) in self.live:
            if s < end and e > start:
                if ob.last_write is not None:
                    b.readers.append(ob.last_write)
                b.readers.extend(ob.readers)
                if s >= start and e <= end:
                    continue
            keep.append((s, e, ob))
        keep.append((start, end, b))
        self.live = keep
        return b

    def alloc(self, name, nelem, dt=F32):
        size = 4 if dt == F32 else 2
        words = (nelem * size + 3) // 4
        words = (words + 7) // 8 * 8
        start = self.top
        self.top += words
        b = self._mk(name, start, words)
        v = self.t[:, start:start + words]
        if dt != F32:
            v = v.bitcast(dt)
        v = v[:, 0:nelem]
        return Tile(v, b)

    def at(self, name, start, nelem, dt=F32):
        size = 4 if dt == F32 else 2
        words = (nelem * size + 3) // 4
        if self.name == "psum":
            bank = start // 512
            assert (start + words - 1) // 512 == bank, "psum tile straddles banks"
            if not hasattr(self, "banks"):
                self.banks = [Buf(f"psbank{i}", exclusive=True) for i in range(8)]
            b = self.banks[bank]
        else:
            b = self._mk(name, start, words)
        v = self.t[:, start:start + words]
        if dt != F32:
            v = v.bitcast(dt)
        v = v[:, 0:nelem]
        return Tile(v, b)


P = 128
D = 2048
KC = 16
T = 4096
SEG = 1024
NSEG = T // SEG
CH = 128
NCH = SEG // CH
DI = 4096
NH = 64
NG = 8
DS = 128
HD = 64
INW = 10304
ALPHA = (2 * 2) ** 0.25
LN_EPS = 1e-5
RMS_EPS = 1e-5
NEG = -30000.0
DIL = ((128, 1), (512, 4), (2048, 16))
NCORES = 4
E_DEPTH = 2
E_STAG = 16
D_DEPTH = 4
D_STAG = 5

SBUF_WORDS = 53000


def build(stop=None):
    nc = bass.Bass("TRN2", target_bir_lowering=False)
    S = Sched(nc)

    def din(name, shape, dt=F32):
        return nc.dram_tensor(name, list(shape), dt, kind="ExternalInput").ap()

    def dscr(name, shape, dt=F32):
        return nc.dram_tensor(name, list(shape), dt, kind="Internal").ap()

    x_d = din("x", [T, D])
    cT_d = din("cT", [P, KC])
    adaw_d = din("ada_w", [2, D, 6144])
    adab_d = din("ada_b", [1, 12288])
    lng_d = din("ln_g", [2, D])
    lnb_d = din("ln_b", [2, D])
    inw_d = din("a_in_w", [D, INW])
    cw_d = din("cw", [P, 48 * 4])
    cb_d = din("cb", [P, 48])
    dtb_d = din("dtb", [1, NH])
    alog_d = din("alog", [1, NH])
    Dc_d = din("Dc", [P, 32])
    ng_d = din("ng", [P, 32])
    ow_d = din("a_out_w", [DI, D])
    kvw_d = din("kv_w", [D, 6144])
    binw_d = din("b_in_w", [D, 4096])
    bow_d = din("b_out_w", [1024, D])
    cst_d = din("cst", [P, 384])
    abias_d = din("abias", [P, 24 * 256])
    out_d = nc.dram_tensor("out", [T, D], F32, kind="ExternalOutput").ap()

    mod_s = dscr("mod_s", [1, 12288])
    yn_s = dscr("yn_s", [T // P, P, 32, P], BF16)
    st_s = dscr("st_s", [NG, P, 512])
    halo_s = dscr("halo_s", [48, P, 4])
    x1_s = dscr("x1_s", [T, D])
    x1T_s = dscr("x1T_s", [KC, P, T], BF16)
    h1T_s = dscr("h1T_s", [KC, P, T], BF16)
    KT_s = dscr("KT_s", [24, P, T], BF16)
    QT_s = dscr("QT_s", [24, P, T], BF16)
    V_s = dscr("V_s", [3, T, 1024], BF16)
    O_s = dscr("O_s", [3, T, 1024])
    L_s = dscr("L_s", [3, T, 8])

    dbg = {}
    dbgd = {}

    with ExitStack() as st:
        sb_t = st.enter_context(nc.sbuf_tensor("arena", [P, SBUF_WORDS], F32))
        ps_t = st.enter_context(nc.psum_tensor("psall", [P, 4096], F32))
        A = Arena(sb_t, SBUF_WORDS, "sbuf")
        PS = Arena(ps_t, 4096, "psum")

        def OP(eng, fn, r=(), w=(), **kw):
            return S.op(eng, fn, reads=[t.buf for t in r], writes=[t.buf for t in w], **kw)

        def DMA(eng, out_ap, in_ap, r=(), w=(), **kw):
            return S.op(eng, lambda e, o=out_ap, i=in_ap: e.dma_start(out=o, in_=i),
                        reads=[t.buf for t in r], writes=[t.buf for t in w], dma=True, **kw)

        def MM(out_t, out_ap, l_t, l_ap, r_t, r_ap, start, stop, extra_r=()):
            return S.op("pe", lambda e, o=out_ap, l=l_ap, r=r_ap, s_=start, p_=stop:
                        e.matmul(o, lhsT=l, rhs=r, start=s_, stop=p_),
                        reads=[l_t.buf, r_t.buf] + [t.buf for t in extra_r], writes=[out_t.buf])

        def TR(out_t, out_ap, in_t, in_ap, id_t, id_ap):
            return S.op("pe", lambda e, o=out_ap, i=in_ap, d=id_ap: e.transpose(o, i, d),
                        reads=[in_t.buf, id_t.buf], writes=[out_t.buf])

        def ACT(out_t, out_ap, in_t, in_ap, func, bias=None, scale=None, r=(), accum=None):
            kw = {}
            if bias is not None:
                kw["bias"] = bias
            if scale is not None:
                kw["scale"] = scale
            if accum is not None:
                kw["accum_out"] = accum
            return S.op("act", lambda e, o=out_ap, i=in_ap, f=func, kw=kw: e.activation(out=o, in_=i, func=f, **kw),
                        reads=[in_t.buf] + [t.buf for t in r], writes=[out_t.buf])

        def TT(eng, out_t, out_ap, a_t, a_ap, b_t, b_ap, op):
            return S.op(eng, lambda e, o=out_ap, a=a_ap, b=b_ap, op=op: e.tensor_tensor(out=o, in0=a, in1=b, op=op),
                        reads=[a_t.buf, b_t.buf], writes=[out_t.buf])

        def TS(eng, out_t, out_ap, a_t, a_ap, s1, s2, op0, op1=None, r=()):
            def f(e, o=out_ap, a=a_ap, s1=s1, s2=s2, op0=op0, op1=op1):
                if op1 is None:
                    return e.tensor_scalar(out=o, in0=a, scalar1=s1, scalar2=None, op0=op0)
                return e.tensor_scalar(out=o, in0=a, scalar1=s1, scalar2=s2, op0=op0, op1=op1)
            return S.op(eng, f, reads=[a_t.buf] + [t.buf for t in r], writes=[out_t.buf])

        def STT(out_t, out_ap, a_t, a_ap, scalar, b_t, b_ap, op0, op1, r=()):
            return S.op("dve", lambda e, o=out_ap, a=a_ap, s_=scalar, b=b_ap, op0=op0, op1=op1:
                        e.scalar_tensor_tensor(out=o, in0=a, scalar=s_, in1=b, op0=op0, op1=op1),
                        reads=[a_t.buf, b_t.buf] + [t.buf for t in r], writes=[out_t.buf])

        def CP(eng, out_t, out_ap, in_t, in_ap):
            if eng == "act":
                return S.op("act", lambda e, o=out_ap, i=in_ap: e.copy(out=o, in_=i), reads=[in_t.buf], writes=[out_t.buf])
            return S.op(eng, lambda e, o=out_ap, i=in_ap: e.tensor_copy(out=o, in_=i), reads=[in_t.buf], writes=[out_t.buf])

        def MEMSET(eng, t, ap, val):
            return S.op(eng, lambda e, a=ap, v=val: e.memset(a, v), writes=[t])

        B_mod = Buf("mod_s")
        B_yn = [Buf(f"yn_s{i}") for i in range(NSEG)]
        B_st = [Buf(f"st_s{i}") for i in range(NG)]
        B_halo = Buf("halo")
        B_x1 = Buf("x1_s")
        B_x1T = Buf("x1T_s")
        B_h1T = Buf("h1T_s")
        B_out = Buf("out")
        B_KT = Buf("KT_s")
        B_QT = Buf("QT_s")
        B_V = Buf("V_s")
        B_O = Buf("O_s")
        B_L = Buf("L_s")

        def DT(b):
            return Tile(None, b)

        cst = A.alloc("cst", 384)
        DMA("sp", cst.ap, cst_d, w=[cst])
        ident = cst.ap[:, 0:128]
        triU = cst.ap[:, 128:256]
        maskb = cst.ap[:, 256:384]
        ones_f = A.alloc("ones_f", 128)
        MEMSET("dve", ones_f.buf, ones_f.ap, 1.0)
        ones_b = A.alloc("ones_b", 128, BF16)
        MEMSET("dve", ones_b.buf, ones_b.ap, 1.0)
        modF = A.alloc("modF", 64)
        base_mark = A.mark()

        m0 = A.mark()
        cT = A.alloc("cT", KC)
        DMA("sp", cT.ap, cT_d, w=[cT])
        scb = A.alloc("scb", KC, BF16)
        ACT(scb, scb.ap, cT, cT.ap, AF.Silu)
        adab = A.alloc("adab", 12288)
        DMA("sp", adab.ap[0:1, :], adab_d, w=[adab])
        modrow = A.alloc("modrow", 12288)
        wring = [A.alloc(f"adaw{i}", KC * 512, BF16) for i in range(3)]
        psrow = [PS.at(f"psrow{i}", i * 512, 512) for i in range(2)]
        it = 0
        p0_issued = [0]

        def p0_issue(upto):
            while p0_issued[0] < min(upto, 24):
                i_ = p0_issued[0]
                p0_issued[0] += 1
                l_, nb_ = divmod(i_, 12)
                wt_ = wring[i_ % 3]
                DMA("pool", wt_.ap.rearrange("p (k n) -> p k n", k=KC),
                    adaw_d[l_].rearrange("(k p) n -> p k n", p=P)[:, :, nb_ * 512:(nb_ + 1) * 512], w=[wt_])
        for l in range(2):
            for nb in range(12):
                p0_issue(it + 3)
                wt = wring[it % 3]
                pr = psrow[it % 2]
                w3 = wt.ap.rearrange("p (k n) -> p k n", k=KC)
                for k in range(KC):
                    MM(pr, pr.ap[0:1, :], scb, scb.ap[:, k:k + 1], wt, w3[:, k, :], k == 0, k == KC - 1)
                c0 = l * 6144 + nb * 512
                TT("dve", modrow, modrow.ap[0:1, c0:c0 + 512], pr, pr.ap[0:1, :], adab, adab.ap[0:1, c0:c0 + 512], ALU.add)
                it += 1
        DMA("sp", mod_s, modrow.ap[0:1, :], r=[modrow], w=[DT(B_mod)])
        psF = PS.at("psF", 1024, 64)
        for l in range(2):
            for j in range(32):
                c0 = l * 6144 + j * 128
                MM(psF, psF.ap[:, l * 32 + j:l * 32 + j + 1], modrow, modrow.ap[0:1, c0:c0 + 128],
                   ones_f, ones_f.ap[0:1, 0:1], True, True)
        CP("dve", modF, modF.ap, psF, psF.ap)
        for l in range(2):
            TS("dve", modF, modF.ap[:, l * 32 + 16:l * 32 + 32], modF, modF.ap[:, l * 32 + 16:l * 32 + 32], 1.0, None, ALU.add)
        A.release(m0)
        if stop == "p0":
            dbg["modF"] = (modF, [P, 64])


        if stop != "p0":
            mA = A.mark()
            cw = A.alloc("cw", 192)
            DMA("sp", cw.ap, cw_d, w=[cw])
            cb = A.alloc("cb", 48)
            DMA("sp", cb.ap, cb_d, w=[cb])
            Dc = A.alloc("Dc", 32)
            DMA("sp", Dc.ap, Dc_d, w=[Dc])
            ng = A.alloc("ng", 32)
            DMA("sp", ng.ap, ng_d, w=[ng])
            dtb = A.alloc("dtb", NH)
            DMA("sp", dtb.ap, dtb_d.partition_broadcast(P), w=[dtb])
            Abc = A.alloc("Abc", NH)
            DMA("sp", Abc.ap, alog_d.partition_broadcast(P), w=[Abc])
            ACT(Abc, Abc.ap, Abc, Abc.ap, AF.Exp)
            TS("dve", Abc, Abc.ap, Abc, Abc.ap, -1.0, None, ALU.mult)
            inw3 = inw_d.rearrange("(k p) n -> p k n", p=P)
            wdt = A.alloc("wdt", KC * NH, BF16)
            wdt3 = wdt.ap.rearrange("p (k n) -> p k n", k=KC)
            DMA("pool", wdt3, inw3[:, :, 10240:10304], w=[wdt])
            hT = A.alloc("hT", KC * SEG, BF16)
            hT3 = hT.ap.rearrange("p (k t) -> p k t", k=KC)
            xt = A.alloc("xt", D)
            wq = [A.alloc(f"wq{i}", KC * 256, BF16) for i in range(3)]
            raw = [A.alloc(f"raw{i}", 3 + SEG) for i in range(2)]
            ctmp = A.alloc("ctmp", SEG)
            sgc = A.alloc("sgc", SEG)
            sgz = [A.alloc(f"sgz{i}", 512) for i in range(2)]
            cnt_z = 0
            sets = [dict(xs=A.alloc(f"xs{i}", 4 * SEG), Bs=A.alloc(f"Bs{i}", SEG), BT=A.alloc(f"BT{i}", SEG, BF16),
                         CT=A.alloc(f"CT{i}", SEG, BF16)) for i in range(2)]
            zs = A.alloc("zs", 4 * SEG, BF16)
            zs3 = zs.ap.rearrange("p (i t) -> p i t", i=4)
            dt_t = A.alloc("dt_t", NCH * NH)
            dtt_t = A.alloc("dtt_t", NCH * NH)
            cum_t = A.alloc("cum_t", NCH * NH)
            eTot = A.alloc("eTot", NCH * NH)
            dt3 = dt_t.ap.rearrange("p (c h) -> p c h", c=NCH)
            dtt3 = dtt_t.ap.rearrange("p (c h) -> p c h", c=NCH)
            cum3 = cum_t.ap.rearrange("p (c h) -> p c h", c=NCH)
            eT3 = eTot.ap.rearrange("p (c h) -> p c h", c=NCH)
            cumT = A.alloc("cumT", SEG)
            St = A.alloc("St", 512)
            Sbf = A.alloc("Sbf", 512, BF16)
            ynT = A.alloc("ynT", 4 * SEG, BF16)
            ynT4 = ynT.ap.rearrange("p (c i t) -> p c i t", c=NCH, i=4)
            halo = A.alloc("halo", 48 * 4)
            halo3 = halo.ap.rearrange("p (j k) -> p j k", j=48)
            MEMSET("pool", halo.buf, halo.ap, 0.0)
            segt = A.alloc("segt", 512)
            Ebc = A.alloc("Ebc", 512)
            pre = [A.alloc(f"pre{i}", NCH * NH) for i in range(4)]
            tmps = []
            for i in range(2):
                tmps.append(dict(
                    xdt=A.alloc(f"xdt{i}", 512, BF16), xtl=A.alloc(f"xtl{i}", 512, BF16),
                    Btm=A.alloc(f"Btm{i}", 128, BF16), CBT=A.alloc(f"CBT{i}", 128),
                    MT=A.alloc(f"MT{i}", 1024, BF16), CkT=A.alloc(f"CkT{i}", 1024, BF16),
                    yg=A.alloc(f"yg{i}", 512), ysq=A.alloc(f"ysq{i}", 512, BF16),
                    rstd=A.alloc(f"rstd{i}", 128), lnv=A.alloc(f"lnv{i}", 128)))

            def v3(ap, a):
                return ap.rearrange("p (a b) -> p a b", a=a)

            def rr(*gens):
                gens = list(gens)
                while gens:
                    for g_ in list(gens):
                        try:
                            next(g_)
                            yield
                        except StopIteration:
                            gens.remove(g_)

            def drain(gen):
                for _ in gen:
                    pass

            cnt_i = 0
            cnt_r = 0
            cnt_w = 0
            cnt_s = 0
            cnt_c = 0
            def xbc_req(g):
                return [(DI + g * 512, 256), (DI + g * 512 + 256, 256), (2 * DI + g * P, P), (2 * DI + 1024 + g * P, P)]
            wreq = []
            for seg_ in range(NSEG):
                wreq += xbc_req(0)
                for g_ in range(NG):
                    wreq += [(g_ * 512, 256), (g_ * 512 + 256, 256)]
                    if g_ + 1 < NG:
                        wreq += xbc_req(g_ + 1)
            wstate = [0]
            for seg in range(NSEG):
                t0 = seg * SEG
                def gen_hT(seg_):
                    tt0_ = seg_ * SEG
                    psT = [PS.at(f"psT{i}", i * 512, 512) for i in range(2)]
                    for j in range(SEG // P):
                        DMA("sp", xt.ap, x_d[tt0_ + j * P:tt0_ + (j + 1) * P, :], w=[xt])
                        for q4 in range(4):
                            pt = psT[(j * 4 + q4) % 2]
                            for i in range(4):
                                k = q4 * 4 + i
                                TR(pt, pt.ap[:, i * P:(i + 1) * P], xt, xt.ap[:, k * P:(k + 1) * P], cst, ident)
                            for i in range(4):
                                k = q4 * 4 + i
                                ACT(hT, hT3[:, k, j * P:(j + 1) * P], pt, pt.ap[:, i * P:(i + 1) * P], AF.Identity,
                                    bias=modF.ap[:, k:k + 1], scale=modF.ap[:, 16 + k:17 + k], r=[modF])
                            yield
                if seg == 0:
                    for _ in gen_hT(0):
                        pass
                psDall = PS.at("psDall", 1536, 512)
                psCum = PS.at("psCum", 3072, 512)
                psTot = PS.at("psTot", 3584, 512)
                psCTa = [PS.at(f"psCTa{i}", 2048 + i * 512, 512) for i in range(2)]
                x1, ax, ex, dA = pre
                dA3 = dA.ap.rearrange("p (c h) -> p c h", c=NCH)
                for c in range(NCH):
                    tsl = slice(c * CH, (c + 1) * CH)
                    for k in range(KC):
                        MM(psDall, psDall.ap[:, c * NH:(c + 1) * NH], hT, hT3[:, k, tsl], wdt, wdt3[:, k, :], k == 0, k == KC - 1)
                TT("dve", x1, x1.ap.rearrange("p (c h) -> p c h", c=NCH), psDall, psDall.ap.rearrange("p (c h) -> p c h", c=NCH),
                   dtb, dtb.ap.unsqueeze(1).to_broadcast([P, NCH, NH]), ALU.add)
                ACT(ax, ax.ap, x1, x1.ap, AF.Abs)
                ACT(ex, ex.ap, ax, ax.ap, AF.Exp, scale=-1.0)
                ACT(ex, ex.ap, ex, ex.ap, AF.Ln, bias=1.0, scale=1.0)
                STT(dt_t, dt_t.ap, x1, x1.ap, 0.0, ex, ex.ap, ALU.max, ALU.add)
                TT("dve", dA, dA3, dt_t, dt3, Abc, Abc.ap.unsqueeze(1).to_broadcast([P, NCH, NH]), ALU.mult)
                for c in range(NCH):
                    MM(psCum, psCum.ap[:, c * NH:(c + 1) * NH], cst, triU, dA, dA3[:, c, :], True, True)
                for c in range(NCH):
                    pct = psCTa[c // 4]
                    MM(pct, pct.ap[0:NH, (c % 4) * P:(c % 4 + 1) * P], dA, dA3[:, c, :], cst, triU, True, True)
                for c in range(NCH):
                    MM(psTot, psTot.ap[:, c * NH:(c + 1) * NH], ones_f, ones_f.ap, dA, dA3[:, c, :], True, True)
                CP("act", cum_t, cum_t.ap, psCum, psCum.ap)
                for i in range(2):
                    CP("act", cumT, cumT.ap[0:NH, i * 512:(i + 1) * 512], psCTa[i], psCTa[i].ap[0:NH, :])
                TT("dve", ax, ax.ap, psTot, psTot.ap, cum_t, cum_t.ap, ALU.subtract)
                ACT(ax, ax.ap, ax, ax.ap, AF.Exp)
                TT("dve", dtt_t, dtt_t.ap, ax, ax.ap, dt_t, dt_t.ap, ALU.mult)
                ACT(eTot, eTot.ap, psTot, psTot.ap, AF.Exp)
                psI = [PS.at(f"psI{i}", i * 512, 512) for i in range(2)]
                psX = PS.at("psX", 1024, 512)
                psB = PS.at("psB", 1536, 128)
                psCB = PS.at("psCB", 1536 + 128, 128)
                psBC = PS.at("psBC", 2048, 512)
                ss = PS.at("ss", 2560, 128)
                psY = PS.at("psY", 3072, 512)
                psS = PS.at("psS", 3584, 512)
                def conv_tile(rw, jch, dsts):
                    TS("dve", ctmp, ctmp.ap, rw, rw.ap[:, 0:SEG], cw.ap[:, jch * 4:jch * 4 + 1], cb.ap[:, jch:jch + 1],
                       ALU.mult, ALU.add, r=[cw, cb])
                    yield
                    for kk in range(1, 4):
                        STT(ctmp, ctmp.ap, rw, rw.ap[:, kk:kk + SEG], cw.ap[:, jch * 4 + kk:jch * 4 + kk + 1],
                            ctmp, ctmp.ap, ALU.mult, ALU.add, r=[cw])
                        yield
                    CP("pool", halo, halo3[:, jch, 0:3], rw, rw.ap[:, SEG:SEG + 3])
                    ACT(sgc, sgc.ap, ctmp, ctmp.ap, AF.Exp, scale=-1.0)
                    yield
                    ACT(sgc, sgc.ap, sgc, sgc.ap, AF.Ln, bias=1.0, scale=1.0)
                    yield
                    ACT(sgc, sgc.ap, sgc, sgc.ap, AF.Exp, scale=-1.0)
                    yield
                    for (dt_, dap) in dsts:
                        TT("pool", dt_, dap, ctmp, ctmp.ap, sgc, sgc.ap, ALU.mult)
                    yield

                def proj_cols(wt, w3, coff, evac):
                    nonlocal cnt_i
                    for blk in range(SEG // 512):
                        ps = psI[cnt_i % 2]
                        cnt_i += 1
                        for k in range(KC):
                            MM(ps, ps.ap, wt, w3[:, k, coff:coff + P], hT, hT3[:, k, blk * 512:(blk + 1) * 512],
                               k == 0, k == KC - 1)
                        evac(ps, blk)
                        yield

                def silu_evac(dst_t, dst_ap, ps):
                    nonlocal cnt_z
                    sg = sgz[cnt_z % 2]
                    cnt_z += 1
                    ACT(sg, sg.ap, ps, ps.ap, AF.Exp, scale=-1.0)
                    ACT(sg, sg.ap, sg, sg.ap, AF.Ln, bias=1.0, scale=1.0)
                    ACT(sg, sg.ap, sg, sg.ap, AF.Exp, scale=-1.0)
                    TT("dve", dst_t, dst_ap, ps, ps.ap, sg, sg.ap, ALU.mult)

                def issue_w(i):
                    if i >= len(wreq) or i < wstate[0]:
                        return
                    assert i == wstate[0]
                    wstate[0] += 1
                    c0_, nc_ = wreq[i]
                    wt_ = wq[i % len(wq)]
                    w3_ = wt_.ap.rearrange("p (k n) -> p k n", k=KC)
                    DMA("pool", w3_[:, :, 0:nc_], inw3[:, :, c0_:c0_ + nc_], w=[wt_])

                def load_w(c0, ncols):
                    nonlocal cnt_w
                    i = cnt_w
                    cnt_w += 1
                    assert wreq[i] == (c0, ncols), (i, wreq[i], c0, ncols)
                    for j_ in range(wstate[0], i + len(wq)):
                        issue_w(j_)
                    wt = wq[i % len(wq)]
                    w3 = wt.ap.rearrange("p (k n) -> p k n", k=KC)
                    return wt, w3

                def gen_xbc(g, ss_):
                    nonlocal cnt_r
                    xs_, Bs_, BT_, CT_ = ss_["xs"], ss_["Bs"], ss_["BT"], ss_["CT"]
                    xs3_ = xs_.ap.rearrange("p (i t) -> p i t", i=4)
                    for hp in range(2):
                        wt, w3 = load_w(DI + g * 512 + hp * 256, 256)
                        for i2 in range(2):
                            i = hp * 2 + i2
                            rw = raw[cnt_r % 2]
                            cnt_r += 1
                            jch = g * 4 + i
                            CP("pool", rw, rw.ap[:, 0:3], halo, halo3[:, jch, 0:3])
                            yield from proj_cols(wt, w3, i2 * P, lambda ps, blk, rw=rw: CP("act", rw, rw.ap[:, 3 + blk * 512:3 + (blk + 1) * 512], ps, ps.ap))
                            yield from conv_tile(rw, jch, [(xs_, xs3_[:, i, :])])
                    for which in range(2):
                        wt, w3 = load_w(DI + DI + which * 1024 + g * P, P)
                        rw = raw[cnt_r % 2]
                        cnt_r += 1
                        jch = 32 + which * 8 + g
                        CP("pool", rw, rw.ap[:, 0:3], halo, halo3[:, jch, 0:3])
                        yield from proj_cols(wt, w3, 0, lambda ps, blk, rw=rw: CP("act", rw, rw.ap[:, 3 + blk * 512:3 + (blk + 1) * 512], ps, ps.ap))
                        if which == 0:
                            yield from conv_tile(rw, jch, [(Bs_, Bs_.ap), (BT_, BT_.ap)])
                        else:
                            yield from conv_tile(rw, jch, [(CT_, CT_.ap)])

                def gen_z(g):
                    for hp in range(2):
                        wt, w3 = load_w(g * 512 + hp * 256, 256)
                        for i2 in range(2):
                            i = hp * 2 + i2
                            yield from proj_cols(wt, w3, i2 * P, lambda ps, blk, i=i: silu_evac(zs, zs3[:, i, blk * 512:(blk + 1) * 512], ps))

                def gen_prep(g, c, ss_, tm):
                    xs_, Bs_, BT_, CT_ = ss_["xs"], ss_["Bs"], ss_["BT"], ss_["CT"]
                    xs3_ = xs_.ap.rearrange("p (i t) -> p i t", i=4)
                    tsl = slice(c * CH, (c + 1) * CH)
                    xdt, xtl, Btm, CBT, MT, CkT = (tm[k_] for k_ in ("xdt", "xtl", "Btm", "CBT", "MT", "CkT"))
                    hs = slice(g * 8, (g + 1) * 8)
                    for i in range(4):
                        TR(psX, psX.ap[:, i * P:(i + 1) * P], xs_, xs3_[:, i, tsl], cst, ident)
                    TR(psB, psB.ap, Bs_, Bs_.ap[:, tsl], cst, ident)
                    MM(psCB, psCB.ap, BT_, BT_.ap[:, tsl], CT_, CT_.ap[:, tsl], True, True)
                    yield
                    TT("dve", xdt, v3(xdt.ap, 8), psX, v3(psX.ap, 8), dt_t, dt3[:, c, hs].unsqueeze(2).to_broadcast([P, 8, HD]), ALU.mult)
                    CP("act", Btm, Btm.ap, psB, psB.ap)
                    CP("act", CBT, CBT.ap, psCB, psCB.ap)
                    yield
                    TT("dve", xtl, v3(xtl.ap, 8), psX, v3(psX.ap, 8), dtt_t, dtt3[:, c, hs].unsqueeze(2).to_broadcast([P, 8, HD]), ALU.mult)
                    yield
                    for hh in range(2):
                        for jj in range(4):
                            head = g * 8 + hh * 4 + jj
                            MM(psBC, psBC.ap[:, jj * P:(jj + 1) * P], cst, ident[0:NH, head:head + 1].to_broadcast([NH, P]),
                               cumT, cumT.ap[0:NH, tsl], True, True)
                        yield
                        h0_ = g * 8 + hh * 4
                        TT("dve", segt, v3(segt.ap, 4), psBC, v3(psBC.ap, 4),
                           cum_t, cum3[:, c, h0_:h0_ + 4].unsqueeze(2).to_broadcast([P, 4, P]), ALU.subtract)
                        ACT(Ebc, Ebc.ap, psBC, psBC.ap, AF.Exp)
                        yield
                        TT("dve", segt, v3(segt.ap, 4), segt, v3(segt.ap, 4),
                           cst, maskb.unsqueeze(1).to_broadcast([P, 4, P]), ALU.add)
                        yield
                        ACT(segt, segt.ap, segt, segt.ap, AF.Exp)
                        TT("pool", CkT, v3(CkT.ap[:, hh * 512:(hh + 1) * 512], 4), Ebc, v3(Ebc.ap, 4),
                           CT_, CT_.ap[:, tsl].unsqueeze(1).to_broadcast([P, 4, P]), ALU.mult)
                        yield
                        TT("dve", MT, v3(MT.ap[:, hh * 512:(hh + 1) * 512], 4), segt, v3(segt.ap, 4),
                           CBT, CBT.ap.unsqueeze(1).to_broadcast([P, 4, P]), ALU.mult)
                        yield

                def gen_tail(g, c, ss_, tm):
                    xs_ = ss_["xs"]
                    xs3_ = xs_.ap.rearrange("p (i t) -> p i t", i=4)
                    tsl = slice(c * CH, (c + 1) * CH)
                    xdt, xtl, Btm, CBT, MT, CkT, yg, ysq, rstd, lnv = (tm[k_] for k_ in
                        ("xdt", "xtl", "Btm", "CBT", "MT", "CkT", "yg", "ysq", "rstd", "lnv"))
                    hs = slice(g * 8, (g + 1) * 8)
                    St3 = v3(St.ap, 8)
                    yg3 = v3(yg.ap, 4)
                    for j in range(8):
                        i, half = j // 2, j % 2
                        yo = psY.ap[half * HD:(half + 1) * HD, i * P:(i + 1) * P]
                        MM(psY, yo, xdt, xdt.ap[:, j * HD:(j + 1) * HD], MT, MT.ap[:, j * P:(j + 1) * P], True, False)
                        MM(psY, yo, Sbf, Sbf.ap[:, j * HD:(j + 1) * HD], CkT, CkT.ap[:, j * P:(j + 1) * P], False, True)
                    MM(psS, psS.ap, Btm, Btm.ap, xtl, xtl.ap, True, True)
                    TT("pool", yg, yg3, xs_, xs3_[:, :, tsl], Dc, Dc.ap[:, g * 4:(g + 1) * 4].unsqueeze(2).to_broadcast([P, 4, P]), ALU.mult)
                    yield
                    TT("dve", St, St3, St, St3, eTot, eT3[:, c, hs].unsqueeze(2).to_broadcast([P, 8, HD]), ALU.mult)
                    yield
                    TT("dve", St, St.ap, psS, psS.ap, St, St.ap, ALU.add)
                    yield
                    CP("pool", Sbf, Sbf.ap, St, St.ap)
                    TT("dve", yg, yg3, psY, v3(psY.ap, 4), yg, yg3, ALU.add)
                    yield
                    TT("dve", yg, yg3, yg, yg3, zs, zs3[:, :, tsl], ALU.mult)
                    yield
                    ACT(ysq, ysq.ap, yg, yg.ap, AF.Square)
                    yield
                    for i in range(4):
                        MM(ss, ss.ap, ones_b, ones_b.ap, ysq, ysq.ap[:, i * P:(i + 1) * P], i == 0, i == 3)
                    yield
                    ACT(lnv, lnv.ap, ss, ss.ap, AF.Ln, bias=RMS_EPS, scale=1.0 / 512)
                    yield
                    ACT(rstd, rstd.ap, lnv, lnv.ap, AF.Exp, scale=-0.5)
                    yield
                    TT("dve", yg, yg3, yg, yg3, rstd, rstd.ap.unsqueeze(1).to_broadcast([P, 4, P]), ALU.mult)
                    yield
                    TT("pool", ynT, ynT4[:, c], yg, yg3, ng, ng.ap[:, g * 4:(g + 1) * 4].unsqueeze(2).to_broadcast([P, 4, P]), ALU.mult)
                    yield

                def gen_scan(g, ss_):
                    nonlocal cnt_c
                    if seg == 0:
                        MEMSET("pool", St.buf, St.ap, 0.0)
                    else:
                        DMA("sp", St.ap, st_s[g], r=[DT(B_st[g])], w=[St])
                    CP("pool", Sbf, Sbf.ap, St, St.ap)
                    tms = []
                    for c in range(NCH):
                        tms.append(tmps[cnt_c % 2])
                        cnt_c += 1
                    yield from gen_prep(g, 0, ss_, tms[0])
                    for c in range(NCH):
                        gens = [gen_tail(g, c, ss_, tms[c])]
                        if c + 1 < NCH:
                            gens.append(gen_prep(g, c + 1, ss_, tms[c + 1]))
                        yield from rr(*gens)
                    DMA("sp", yn_s[seg * NCH:(seg + 1) * NCH, :, g * 4:(g + 1) * 4, :].rearrange("c p i t -> p c i t"), ynT4, r=[ynT], w=[DT(B_yn[seg])], nowaw=True)
                    if seg < NSEG - 1:
                        DMA("sp", st_s[g], St.ap, r=[St], w=[DT(B_st[g])])

                def chain(*gens):
                    for g_ in gens:
                        yield from g_

                if seg == 0:
                    drain(gen_xbc(0, sets[0]))
                for g in range(NG):
                    nxt = []
                    if g + 1 < NG:
                        nxt.append(gen_xbc(g + 1, sets[(g + 1) % 2]))
                    elif seg + 1 < NSEG:
                        nxt.append(gen_hT(seg + 1))
                        nxt.append(gen_xbc(0, sets[0]))
                    drain(rr(chain(gen_z(g), *nxt), gen_scan(g, sets[g % 2])))
            if stop == "pA":
                dbgd["yn_s"] = (yn_s.rearrange("c p i t -> c p (i t)"), B_yn, [T // P, P, 32 * P], BF16)
            A.release(mA)


        if stop not in ("p0", "pA"):
            mB = A.mark()
            lng = A.alloc("lng", D)
            lnb = A.alloc("lnb", D)
            g1b0 = A.alloc("g1b0", D)
            DMA("sp", g1b0.ap, mod_s[0:1, 4096:6144].partition_broadcast(P), r=[DT(B_mod)], w=[g1b0])
            TS("pool", g1b0, g1b0.ap, g1b0, g1b0.ap, 1.0, None, ALU.add)
            g1bc = [g1b0, None]
            DMA("sp", lng.ap, lng_d[0:1, :].partition_broadcast(P), w=[lng])
            DMA("sp", lnb.ap, lnb_d[0:1, :].partition_broadcast(P), w=[lnb])
            ow = A.alloc("ow", 32 * D, BF16)
            ow3 = ow.ap.rearrange("p (i n) -> p i n", i=32)
            owd = ow_d.rearrange("(i p) n -> p i n", p=P)
            for q4 in range(4):
                DMA("pool", ow3[:, q4 * 8:(q4 + 1) * 8, :], owd[:, q4 * 8:(q4 + 1) * 8, :], w=[ow], nowaw=True)
            ynt = [A.alloc(f"ynt{i}", 32 * P, BF16) for i in range(2)]
            xtB = [A.alloc(f"xtB{i}", D) for i in range(1)]
            rBs = [A.alloc(f"rB{i}", D) for i in range(2)]
            stg_x = A.alloc("stg_x", KC * P, BF16)
            stg_h = A.alloc("stg_h", KC * P, BF16)
            bst = A.alloc("bst", 24)
            mv = A.alloc("mv", 2)
            rs = A.alloc("rs", 1)
            nmr = A.alloc("nmr", 1)
            psO = [PS.at(f"psO{i}", i * 512, 512) for i in range(4)]
            psTB = [PS.at(f"psTB{i}", 2048 + i * 512, 512) for i in range(4)]
            NT = T // P

            def loadB(j):
                yt = ynt[j % 2]
                DMA("sp", yt.ap, yn_s[j].rearrange("p i t -> p (i t)"),
                    r=[DT(B_yn[(j * P) // SEG])], w=[yt])
                if j == 0:
                    DMA("sp", xtB[0].ap, x_d[0:P, :], w=[xtB[0]])

            def outproj(j):
                yt = ynt[j % 2]
                y3 = yt.ap.rearrange("p (i t) -> p i t", i=32)
                for n in range(4):
                    for i in range(32):
                        MM(psO[n], psO[n].ap, yt, y3[:, i, :], ow, ow3[:, i, n * 512:(n + 1) * 512], i == 0, i == 31)

            def epi1(j):
                rB = rBs[j % 2]
                for n in range(4):
                    TT("dve", rB, rB.ap[:, n * 512:(n + 1) * 512], psO[n], psO[n].ap, g1bc[0], g1bc[0].ap[:, n * 512:(n + 1) * 512], ALU.mult)

            def epi2(j, lidx, dst_d, dst_b, xsrc, do_T):
                rB = rBs[j % 2]
                STT(rB, rB.ap, xsrc, xsrc.ap, ALPHA, rB, rB.ap, ALU.mult, ALU.add)
                if j + 1 < NT:
                    DMA("sp", xtB[0].ap, x_d[(j + 1) * P:(j + 2) * P, :], w=[xtB[0]])
                for n in range(4):
                    S.op("dve", lambda e, o=bst.ap[:, n * 6:(n + 1) * 6], i=rB.ap[:, n * 512:(n + 1) * 512]: e.bn_stats(out=o, in_=i),
                         reads=[rB.buf], writes=[bst.buf])
                S.op("dve", lambda e, o=mv.ap, i=bst.ap: e.bn_aggr(out=o, in_=i), reads=[bst.buf], writes=[mv.buf])
                ACT(rs, rs.ap, mv, mv.ap[:, 1:2], AF.Ln, bias=LN_EPS, scale=1.0)
                ACT(rs, rs.ap, rs, rs.ap, AF.Exp, scale=-0.5)
                TS("dve", nmr, nmr.ap, mv, mv.ap[:, 0:1], rs.ap, -1.0, ALU.mult, ALU.mult, r=[rs])
                ACT(rB, rB.ap, rB, rB.ap, AF.Identity, bias=nmr.ap, scale=rs.ap, r=[nmr, rs])
                TT("dve", rB, rB.ap, rB, rB.ap, lng, lng.ap, ALU.mult)
                TT("dve", rB, rB.ap, rB, rB.ap, lnb, lnb.ap, ALU.add)
                DMA("act", dst_d[j * P:(j + 1) * P, :], rB.ap, r=[rB], w=[DT(dst_b)], nowaw=True)
                if do_T:
                    sx3 = stg_x.ap.rearrange("p (k t) -> p k t", k=KC)
                    sh3 = stg_h.ap.rearrange("p (k t) -> p k t", k=KC)
                    for q4 in range(4):
                        pt = psTB[q4]
                        for i in range(4):
                            k = q4 * 4 + i
                            TR(pt, pt.ap[:, i * P:(i + 1) * P], rB, rB.ap[:, k * P:(k + 1) * P], cst, ident)
                        CP("act", stg_x, stg_x.ap[:, q4 * 512:(q4 + 1) * 512], pt, pt.ap)
                        for i in range(4):
                            k = q4 * 4 + i
                            ACT(stg_h, sh3[:, k, :], pt, pt.ap[:, i * P:(i + 1) * P], AF.Identity,
                                bias=modF.ap[:, 32 + k:32 + k + 1], scale=modF.ap[:, 32 + 16 + k:32 + 17 + k], r=[modF])
                    DMA("act", x1T_s[:, :, j * P:(j + 1) * P].rearrange("k p t -> p k t"), sx3, r=[stg_x], w=[DT(B_x1T)], nowaw=True)
                    DMA("act", h1T_s[:, :, j * P:(j + 1) * P].rearrange("k p t -> p k t"), sh3, r=[stg_h], w=[DT(B_h1T)], nowaw=True)

            loadB(0)
            outproj(0)
            for j in range(NT):
                if j + 1 < NT:
                    loadB(j + 1)
                epi1(j)
                if j + 1 < NT:
                    outproj(j + 1)
                epi2(j, 0, x1_s, B_x1, xtB[0], True)
            if stop == "pB":
                dbgd["x1_s"] = (x1_s.rearrange("(a t) d -> a t d", a=32), [B_x1], [32, P, D], F32)
                dbgd["h1T_s"] = (h1T_s, [B_h1T], [KC, P, T], BF16)
            A.release(mB)


        if stop not in ("p0", "pA", "pB"):
            mC = A.mark()
            wr = [A.alloc(f"wC{i}", KC * 512, BF16) for i in range(3)]
            creq = []
            for g_ in range(3):
                for half_ in range(2):
                    creq.append((kvw_d, 0 + g_ * 1024 + half_ * 512))
            for g_ in range(3):
                for half_ in range(2):
                    creq.append((kvw_d, 3072 + g_ * 1024 + half_ * 512))
            for g_ in range(3):
                for half_ in range(2):
                    creq.append((binw_d, 0 + g_ * 1024 + half_ * 512))
            cstate = [0]

            def issue_c(i):
                if i >= len(creq) or i < cstate[0]:
                    return
                cstate[0] += 1
                src_, c0_ = creq[i]
                wt_ = wr[i % 3]
                DMA("pool", wt_.ap.rearrange("p (k n) -> p k n", k=KC), src_.rearrange("(k p) n -> p k n", p=P)[:, :, c0_:c0_ + 512], w=[wt_])

            def get_w(c0_expect):
                nonlocal cw_i
                i = cw_i
                cw_i += 1
                assert creq[i][1] == c0_expect, (i, creq[i][1], c0_expect)
                for j_ in range(cstate[0], i + 3):
                    issue_c(j_)
                wt_ = wr[i % 3]
                return wt_, wt_.ap.rearrange("p (k n) -> p k n", k=KC)
            kto = [A.alloc(f"kto{i}", T, BF16) for i in range(2)]
            vo = [A.alloc(f"vo{i}", 512, BF16) for i in range(2)]
            psC = [PS.at(f"psC{i}", i * 512, 512) for i in range(4)]
            cw_i = 0
            ck_i = 0
            cv_i = 0
            cp_i = 0
            kvw3 = kvw_d.rearrange("(k p) n -> p k n", p=P)
            binw3 = binw_d.rearrange("(k p) n -> p k n", p=P)

            def featmajor_proj(src, src3, wsrc3, col0, dst_s, dst_b):
                nonlocal cw_i, ck_i, cp_i
                for g in range(3):
                    d = DIL[g][1]
                    for half in range(2):
                        c0 = col0 + g * 1024 + half * 512
                        wt, w3 = get_w(c0)
                        for hh in range(4):
                            h = half * 4 + hh
                            ko = kto[ck_i % 2]
                            ck_i += 1
                            ko3 = ko.ap.rearrange("p (r m) -> p r m", r=d)
                            for blk in range(T // 512):
                                ps = psC[cp_i % 4]
                                cp_i += 1
                                for k in range(KC):
                                    MM(ps, ps.ap, wt, w3[:, k, hh * P:(hh + 1) * P], src, src3[:, k, blk * 512:(blk + 1) * 512],
                                       k == 0, k == KC - 1)
                                mpb = 512 // d
                                oap = ko3[:, :, blk * mpb:(blk + 1) * mpb]
                                iap = ps.ap.rearrange("p (m r) -> p r m", r=d)
                                CP("act" if cp_i % 2 == 0 else "dve", ko, oap, ps, iap)
                            DMA("sp", dst_s[g * 8 + h], ko.ap, r=[ko], w=[DT(dst_b)], nowaw=True)

            xT = A.alloc("xT_C", KC * T, BF16)
            xT3 = xT.ap.rearrange("p (k t) -> p k t", k=KC)
            for k in range(KC):
                DMA("sp", xT3[:, k, :], x1T_s[k], r=[DT(B_x1T)], w=[xT], nowaw=True)
            featmajor_proj(xT, xT3, kvw3, 0, KT_s, B_KT)
            for g in range(3):
                d = DIL[g][1]
                for half in range(2):
                    c0 = 3072 + g * 1024 + half * 512
                    wt, w3 = get_w(c0)
                    for tt in range(T // P):
                        r_ = (tt * P) // (T // d)
                        mb = ((tt * P) % (T // d)) // P
                        base = mb * P * d + r_
                        ps = psC[cp_i % 4]
                        cp_i += 1
                        for k in range(KC):
                            MM(ps, ps.ap, xT, xT3[:, k, base:base + (P - 1) * d + 1:d], wt, w3[:, k, :], k == 0, k == KC - 1)
                        v_ = vo[cv_i % 2]
                        cv_i += 1
                        CP("act" if cv_i % 2 == 0 else "dve", v_, v_.ap, ps, ps.ap)
                        DMA("sp", V_s[g, tt * P:(tt + 1) * P, half * 512:(half + 1) * 512], v_.ap, r=[v_], w=[DT(B_V)], nowaw=True)
            for k in range(KC):
                DMA("sp", xT3[:, k, :], h1T_s[k], r=[DT(B_h1T)], w=[xT], nowaw=True)
            featmajor_proj(xT, xT3, binw3, 0, QT_s, B_QT)
            if stop == "pC":
                dbgd["KT_s"] = (KT_s, [B_KT], [24, P, T], BF16)
                dbgd["QT_s"] = (QT_s, [B_QT], [24, P, T], BF16)
                dbgd["V_s"] = (V_s.rearrange("g (a t) c -> (g a) t c", a=8), [B_V], [24, 512, 1024], BF16)
            A.release(mC)


        if stop not in ("p0", "pA", "pB", "pC"):
            mD = A.mark()
            SCALE = float(DS) ** -0.5
            ab = A.alloc("abias", 24 * 256)
            ab3 = ab.ap.rearrange("p (a k) -> p a k", a=24)
            DMA("sp", ab.ap, abias_d, w=[ab])
            negt = A.alloc("negt", P)
            MEMSET("pool", negt.buf, negt.ap, NEG)
            identb = A.alloc("identb", P, BF16)
            CP("pool", identb, identb.ap, cst, ident)
            NR = 3
            qt_r = [A.alloc(f"qt{i}", 8 * P, BF16) for i in range(NR)]
            kt_r = [A.alloc(f"kt{i}", 8 * 256, BF16) for i in range(NR)]
            v_r = [A.alloc(f"vt{i}", 2 * 1024, BF16) for i in range(NR)]
            o_r = [A.alloc(f"ot{i}", 1024) for i in range(NR)]
            l_r = [A.alloc(f"lt{i}", 8) for i in range(NR)]
            NHS = 6
            hsets = [dict(sbt=A.alloc(f"sbt{i}", 1024), pn=A.alloc(f"pn{i}", 1024, BF16), ptS=A.alloc(f"ptS{i}", 8 * P, BF16),
                          mx=A.alloc(f"mx{i}", 4), nmx=A.alloc(f"nmx{i}", 4), den=A.alloc(f"den{i}", 4),
                          rden=A.alloc(f"rden{i}", 4), lden=A.alloc(f"lden{i}", 4)) for i in range(NHS)]
            psSc = [PS.at(f"psSc{i}", i * 512, 512) for i in range(4)]
            psPT = [[PS.at(f"psPT{h}_{i}", 2048 + h * 512 + i * 256, 512, BF16) for i in range(2)] for h in range(2)]
            psOD = [PS.at(f"psOD{i}", 3072 + i * 512, 512) for i in range(2)]
            tiles = []
            for g in range(3):
                d = DIL[g][1]
                M = T // d
                for r_ in range(d):
                    for mb in range(M // P):
                        tiles.append((g, d, M, r_, mb))

            def gen_half(ti, half, hset):
                g, d, M, r_, mb = tiles[ti]
                Og = O_s[g].rearrange("(m r) c -> r m c", r=d)
                Lg = L_s[g].rearrange("(m r) c -> r m c", r=d)
                base = r_ * M + mb * P
                kb = base - P if mb > 0 else base
                qt = qt_r[ti % NR]
                kt = kt_r[ti % NR]
                vt = v_r[ti % NR]
                ot = o_r[ti % NR]
                lt = l_r[ti % NR]
                sbt, pn, ptS, mx, nmx, den, rden, lden = (hset[k_] for k_ in ("sbt", "pn", "ptS", "mx", "nmx", "den", "rden", "lden"))
                q3 = qt.ap.rearrange("p (h t) -> p h t", h=8)
                k3 = kt.ap.rearrange("p (h t) -> p h t", h=8)
                v3_ = vt.ap.rearrange("p (a c) -> p a c", a=2)
                if half == 0:
                    DMA("sp", q3, QT_s[g * 8:(g + 1) * 8, :, base:base + P].rearrange("h p t -> p h t"), r=[DT(B_QT)], w=[qt])
                    if mb > 0:
                        DMA("sp", k3, KT_s[g * 8:(g + 1) * 8, :, kb:kb + 256].rearrange("h p t -> p h t"), r=[DT(B_KT)], w=[kt])
                        DMA("sp", v3_, V_s[g, kb:kb + 256, :].rearrange("(a p) c -> p a c", a=2), r=[DT(B_V)], w=[vt])
                    else:
                        DMA("sp", k3[:, :, P:256], KT_s[g * 8:(g + 1) * 8, :, base:base + P].rearrange("h p t -> p h t"), r=[DT(B_KT)], w=[kt])
                        DMA("sp", v3_[:, 1, :], V_s[g, base:base + P, :], r=[DT(B_V)], w=[vt])
                    yield
                s3 = sbt.ap.rearrange("p (h k) -> p h k", h=4)
                for hh in range(4):
                    h = half * 4 + hh
                    ps = psSc[half * 2 + hh // 2]
                    pso = ps.ap[:, (hh % 2) * 256:(hh % 2 + 1) * 256]
                    if mb > 0:
                        MM(ps, pso, qt, q3[:, h, :], kt, k3[:, h, :], True, True)
                        STT(sbt, s3[:, hh, :], ps, pso, SCALE, ab, ab3[:, g * 8 + h, :], ALU.mult, ALU.add)
                    else:
                        MM(ps, pso[:, P:256], qt, q3[:, h, :], kt, k3[:, h, P:256], True, True)
                        STT(sbt, s3[:, hh, P:256], ps, pso[:, P:256], SCALE, ab, ab3[:, g * 8 + h, P:256], ALU.mult, ALU.add)
                    yield
                if mb == 0:
                    S.op("pool", lambda e, a=s3[:, :, 0:P]: e.memset(a, NEG), writes=[sbt.buf])
                S.op("dve", lambda e, o=mx.ap, i=s3: e.tensor_reduce(out=o, in_=i, axis=AX.X, op=ALU.max),
                     reads=[sbt.buf], writes=[mx.buf])
                yield
                TS("dve", nmx, nmx.ap, mx, mx.ap, -1.0, None, ALU.mult)
                yield
                for hh in range(4):
                    ACT(sbt, s3[:, hh, :], sbt, s3[:, hh, :], AF.Exp, bias=nmx.ap[:, hh:hh + 1], r=[nmx])
                    yield
                S.op("dve", lambda e, o=den.ap, i=s3: e.tensor_reduce(out=o, in_=i, axis=AX.X, op=ALU.add),
                     reads=[sbt.buf], writes=[den.buf])
                yield
                S.op("dve", lambda e, o=rden.ap, i=den.ap: e.reciprocal(out=o, in_=i), reads=[den.buf], writes=[rden.buf])
                ACT(lden, lden.ap, den, den.ap, AF.Ln)
                yield
                TT("dve", pn, pn.ap.rearrange("p (h k) -> p h k", h=4), sbt, s3,
                   rden, rden.ap.unsqueeze(2).to_broadcast([P, 4, 256]), ALU.mult)
                yield
                TT("dve", lt, lt.ap[:, half * 4:(half + 1) * 4], lden, lden.ap, mx, mx.ap, ALU.add)
                p3 = pn.ap.rearrange("p (h k) -> p h k", h=4)
                pts3 = ptS.ap.rearrange("p (a q) -> p a q", a=8)
                kts = (0, 1) if mb > 0 else (1,)
                pps = psPT[half]
                for hh in range(4):
                    for kt_i in kts:
                        pp = pps[hh // 2]
                        slot = (hh % 2) * 2 + kt_i
                        TR(pp, pp.ap[:, slot * P:(slot + 1) * P], pn, p3[:, hh, kt_i * P:(kt_i + 1) * P], identb, identb.ap)
                yield
                for i2 in range(2):
                    if mb > 0:
                        CP("act", ptS, ptS.ap[:, i2 * 512:(i2 + 1) * 512], pps[i2], pps[i2].ap)
                    else:
                        pv = pps[i2].ap.rearrange("p (a q) -> p a q", a=4)
                        CP("act", ptS, pts3[:, i2 * 4 + 1:i2 * 4 + 4:2, :], pps[i2], pv[:, 1:4:2, :])
                yield
                po = psOD[half]
                for hh in range(4):
                    h = half * 4 + hh
                    for kt_i in kts:
                        MM(po, po.ap[:, hh * P:(hh + 1) * P], ptS, pts3[:, hh * 2 + kt_i, :], vt, v3_[:, kt_i, h * P:(h + 1) * P],
                           kt_i == kts[0], kt_i == kts[-1])
                yield
                CP("act", ot, ot.ap[:, half * 512:(half + 1) * 512], po, po.ap)
                yield
                if half == 1:
                    DMA("act", Og[r_, mb * P:(mb + 1) * P, :], ot.ap, r=[ot], w=[DT(B_O)], nowaw=True)
                    DMA("act", Lg[r_, mb * P:(mb + 1) * P, :], lt.ap, r=[lt], w=[DT(B_L)], nowaw=True)

            work = [(ti, half) for ti in range(len(tiles)) for half in range(2)]
            active = []
            nxt_w = 0
            step = 0
            while nxt_w < len(work) or active:
                if nxt_w < len(work) and len(active) < D_DEPTH and (not active or step >= D_STAG):
                    ti, half = work[nxt_w]
                    active.append(gen_half(ti, half, hsets[nxt_w % NHS]))
                    nxt_w += 1
                    step = 0
                for g_ in list(active):
                    try:
                        next(g_)
                    except StopIteration:
                        active.remove(g_)
                step += 1
            if stop == "pD":
                dbgd["O_s"] = (O_s.rearrange("g (a t) c -> (g a) t c", a=8), [B_O], [24, 512, 1024], F32)
                dbgd["L_s"] = (L_s, [B_L], [3, T, 8], F32)
            A.release(mD)


        if stop not in ("p0", "pA", "pB", "pC", "pD"):
            mE = A.mark()
            g1b1 = A.alloc("g1b1", D)
            g1bc = [None, g1b1]
            DMA("sp", g1bc[1].ap, mod_s[0:1, 6144 + 4096:6144 + 6144].partition_broadcast(P), r=[DT(B_mod)], w=[g1bc[1]])
            TS("pool", g1bc[1], g1bc[1].ap, g1bc[1], g1bc[1].ap, 1.0, None, ALU.add)
            lng1 = A.alloc("lng1", D)
            lnb1 = A.alloc("lnb1", D)
            DMA("sp", lng1.ap, lng_d[1:2, :].partition_broadcast(P), w=[lng1])
            DMA("sp", lnb1.ap, lnb_d[1:2, :].partition_broadcast(P), w=[lnb1])
            wz = A.alloc("wz", KC * 1024, BF16)
            wz3 = wz.ap.rearrange("p (k n) -> p k n", k=KC)
            binw3e = binw_d.rearrange("(k p) n -> p k n", p=P)
            DMA("pool", wz3, binw3e[:, :, 3072:4096], w=[wz])
            bow = A.alloc("bow", 8 * D, BF16)
            bow3 = bow.ap.rearrange("p (i n) -> p i n", i=8)
            DMA("pool", bow3, bow_d.rearrange("(i p) n -> p i n", p=P), w=[bow])
            og_r = [[A.alloc(f"og{i}_{g}", 1024) for g in range(3)] for i in range(2)]
            lt_r = [A.alloc(f"ltE{i}", 24) for i in range(2)]
            h1_r = [A.alloc(f"h1E{i}", KC * P, BF16) for i in range(2)]
            x1_r = [A.alloc(f"x1E{i}", D) for i in range(2)]
            tE = [dict(szt=A.alloc(f"szt{i}", 1024), sg=[A.alloc(f"sgE{i}_{n}", 512) for n in range(2)], acc=A.alloc(f"accE{i}", 1024),
                       tmp=A.alloc(f"tmpE{i}", 1024), ogT=A.alloc(f"ogT{i}", 8 * P, BF16), rE=A.alloc(f"rE{i}", D),
                       lmx=A.alloc(f"lmx{i}", 8), lex=A.alloc(f"lex{i}", 24), lsm=A.alloc(f"lsm{i}", 8),
                       bst=A.alloc(f"bstE{i}", 24), mv=A.alloc(f"mvE{i}", 2), rs=A.alloc(f"rsE{i}", 1), nmr=A.alloc(f"nmrE{i}", 1))
                  for i in range(2)]
            psZ = [PS.at(f"psZ{i}", i * 512, 512) for i in range(2)]
            psTE = [PS.at(f"psTE{i}", 1024 + i * 512, 512) for i in range(2)]
            psOE = [PS.at(f"psOE{i}", 2048 + i * 512, 512) for i in range(4)]
            NT = T // P

            def rrE(*gens):
                gens = list(gens)
                while gens:
                    for g_ in list(gens):
                        try:
                            next(g_)
                        except StopIteration:
                            gens.remove(g_)

            def gen_E(j):
                ogs = og_r[j % 2]
                ltE = lt_r[j % 2]
                h1t = h1_r[j % 2]
                x1t = x1_r[j % 2]
                te = tE[j % 2]
                szt, acc, tmpE, ogT, rE = te["szt"], te["acc"], te["tmp"], te["ogT"], te["rE"]
                lmx, lex, lsm, bstE, mvE, rsE, nmrE = te["lmx"], te["lex"], te["lsm"], te["bst"], te["mv"], te["rs"], te["nmr"]
                l3 = ltE.ap.rearrange("p (g h) -> p g h", g=3)
                h13 = h1t.ap.rearrange("p (k t) -> p k t", k=KC)
                for g in range(3):
                    DMA("sp", ogs[g].ap, O_s[g, j * P:(j + 1) * P, :], r=[DT(B_O)], w=[ogs[g]])
                DMA("sp", l3, L_s[:, j * P:(j + 1) * P, :].rearrange("g p h -> p g h"), r=[DT(B_L)], w=[ltE])
                DMA("sp", h13, h1T_s[:, :, j * P:(j + 1) * P].rearrange("k p t -> p k t"), r=[DT(B_h1T)], w=[h1t])
                DMA("sp", x1t.ap, x1_s[j * P:(j + 1) * P, :], r=[DT(B_x1)], w=[x1t])
                yield
                TT("dve", lmx, lmx.ap, ltE, l3[:, 0, :], ltE, l3[:, 1, :], ALU.max)
                yield
                TT("dve", lmx, lmx.ap, lmx, lmx.ap, ltE, l3[:, 2, :], ALU.max)
                yield
                e3 = lex.ap.rearrange("p (g h) -> p g h", g=3)
                TT("dve", lex, e3, ltE, l3, lmx, lmx.ap.unsqueeze(1).to_broadcast([P, 3, 8]), ALU.subtract)
                yield
                ACT(lex, lex.ap, lex, lex.ap, AF.Exp)
                yield
                TT("dve", lsm, lsm.ap, lex, e3[:, 0, :], lex, e3[:, 1, :], ALU.add)
                yield
                TT("dve", lsm, lsm.ap, lsm, lsm.ap, lex, e3[:, 2, :], ALU.add)
                yield
                S.op("dve", lambda e, o=lsm.ap, i=lsm.ap: e.reciprocal(out=o, in_=i), reads=[lsm.buf], writes=[lsm.buf])
                yield
                TT("dve", lex, e3, lex, e3, lsm, lsm.ap.unsqueeze(1).to_broadcast([P, 3, 8]), ALU.mult)
                yield
                a3 = acc.ap.rearrange("p (h e) -> p h e", h=8)
                t3 = tmpE.ap.rearrange("p (h e) -> p h e", h=8)
                TT("dve", acc, a3, ogs[0], ogs[0].ap.rearrange("p (h e) -> p h e", h=8), lex, e3[:, 0, :].unsqueeze(2).to_broadcast([P, 8, P]), ALU.mult)
                yield
                for g in (1, 2):
                    TT("pool", tmpE, t3, ogs[g], ogs[g].ap.rearrange("p (h e) -> p h e", h=8), lex, e3[:, g, :].unsqueeze(2).to_broadcast([P, 8, P]), ALU.mult)
                    yield
                    TT("pool", acc, acc.ap, acc, acc.ap, tmpE, tmpE.ap, ALU.add)
                    yield
                for n in range(2):
                    sgE = te["sg"][n]
                    for k in range(KC):
                        MM(psZ[n], psZ[n].ap, h1t, h13[:, k, :], wz, wz3[:, k, n * 512:(n + 1) * 512], k == 0, k == KC - 1)
                    ACT(sgE, sgE.ap, psZ[n], psZ[n].ap, AF.Exp, scale=-1.0)
                    yield
                    ACT(sgE, sgE.ap, sgE, sgE.ap, AF.Ln, bias=1.0, scale=1.0)
                    yield
                    ACT(sgE, sgE.ap, sgE, sgE.ap, AF.Exp, scale=-1.0)
                    yield
                    TT("dve", szt, szt.ap[:, n * 512:(n + 1) * 512], psZ[n], psZ[n].ap, sgE, sgE.ap, ALU.mult)
                    yield
                TT("dve", acc, acc.ap, acc, acc.ap, szt, szt.ap, ALU.mult)
                yield
                ogT3 = ogT.ap.rearrange("p (i t) -> p i t", i=8)
                for q2 in range(2):
                    pt = psTE[q2]
                    for i in range(4):
                        TR(pt, pt.ap[:, i * P:(i + 1) * P], acc, acc.ap[:, (q2 * 4 + i) * P:(q2 * 4 + i + 1) * P], cst, ident)
                    CP("act", ogT, ogT.ap[:, q2 * 512:(q2 + 1) * 512], pt, pt.ap)
                    yield
                for n in range(4):
                    for i in range(8):
                        MM(psOE[n], psOE[n].ap, ogT, ogT3[:, i, :], bow, bow3[:, i, n * 512:(n + 1) * 512], i == 0, i == 7)
                    TT("dve", rE, rE.ap[:, n * 512:(n + 1) * 512], psOE[n], psOE[n].ap, g1bc[1], g1bc[1].ap[:, n * 512:(n + 1) * 512], ALU.mult)
                    yield
                STT(rE, rE.ap, x1t, x1t.ap, ALPHA, rE, rE.ap, ALU.mult, ALU.add)
                yield
                for n in range(4):
                    S.op("dve", lambda e, o=bstE.ap[:, n * 6:(n + 1) * 6], i=rE.ap[:, n * 512:(n + 1) * 512]: e.bn_stats(out=o, in_=i),
                         reads=[rE.buf], writes=[bstE.buf])
                yield
                S.op("dve", lambda e, o=mvE.ap, i=bstE.ap: e.bn_aggr(out=o, in_=i), reads=[bstE.buf], writes=[mvE.buf])
                yield
                ACT(rsE, rsE.ap, mvE, mvE.ap[:, 1:2], AF.Ln, bias=LN_EPS, scale=1.0)
                yield
                ACT(rsE, rsE.ap, rsE, rsE.ap, AF.Exp, scale=-0.5)
                yield
                TS("dve", nmrE, nmrE.ap, mvE, mvE.ap[:, 0:1], rsE.ap, -1.0, ALU.mult, ALU.mult, r=[rsE])
                yield
                ACT(rE, rE.ap, rE, rE.ap, AF.Identity, bias=nmrE.ap, scale=rsE.ap, r=[nmrE, rsE])
                yield
                TT("pool", rE, rE.ap, rE, rE.ap, lng1, lng1.ap, ALU.mult)
                yield
                TT("dve", rE, rE.ap, rE, rE.ap, lnb1, lnb1.ap, ALU.add)
                yield
                DMA("act", out_d[j * P:(j + 1) * P, :], rE.ap, r=[rE], w=[DT(B_out)], nowaw=True)

            def stag(j):
                g_ = gen_E(j)
                return g_
            active = []
            nxt_j = 0
            step = 0
            while nxt_j < NT or active:
                if nxt_j < NT and len(active) < E_DEPTH and (not active or step >= E_STAG):
                    active.append(gen_E(nxt_j))
                    nxt_j += 1
                    step = 0
                for g_ in list(active):
                    try:
                        next(g_)
                    except StopIteration:
                        active.remove(g_)
                step += 1
            A.release(mE)
            final_out = True

        A.release(base_mark)

        finals = [B_out] if stop in (None, "pE") else []
        for name, (t, shape) in dbg.items():
            od = nc.dram_tensor("dbg_" + name, list(shape), F32, kind="ExternalOutput").ap()
            ob = Buf("dbg_" + name)
            DMA("sp", od, t.ap, r=[t], w=[DT(ob)])
            finals.append(ob)
        for name, (src, bufs, shape, dt_) in dbgd.items():
            od = nc.dram_tensor("dbg_" + name, list(shape), dt_, kind="ExternalOutput").ap()
            ob = Buf("dbg_" + name)
            for i0 in range(shape[0]):
                S.op("sp", lambda e, o=od[i0], i=src[i0]: e.dma_start(out=o, in_=i), reads=list(bufs), writes=[ob], dma=True, nowaw=True)
            finals.append(ob)
        S.finish(finals)
        S.emit(st)
    print("sched stats", S.stats(), "sbuf peak words", A.peak, "n dma sems", len(S.dma_bufs))
    return nc, list(dbg.keys()) + list(dbgd.keys())


def phaseA(nc, S, A, PS, env):
    pass


def make_consts():
    cst = np.zeros((P, 384), np.float32)
    cst[:, 0:128] = np.eye(P, dtype=np.float32)
    i = np.arange(P)
    cst[:, 128:256] = (i[:, None] <= i[None, :]).astype(np.float32)
    cst[:, 256:384] = np.where(i[None, :] >= i[:, None], 0.0, NEG).astype(np.float32)
    n = 24
    slopes = (2.0 ** (-8.0 * np.arange(1, n + 1) / n)).reshape(3, 8)
    q = np.arange(128)[:, None]
    k = np.arange(256)[None, :]
    delta = q + 128 - k
    valid = (delta >= 0) & (delta <= 128)
    ab = np.zeros((P, 24, 256), np.float32)
    for g in range(3):
        for h in range(8):
            ab[:, g * 8 + h, :] = np.where(valid, -slopes[g, h] * delta * DIL[g][1], NEG)
    return cst, ab.reshape(P, 24 * 256)


def core_inputs(b, inputs, cst, ab):
    f = lambda a: np.ascontiguousarray(a, dtype=np.float32)
    cw = inputs["a_conv_w"][0]
    cwl = cw.reshape(4, 48, P).transpose(2, 1, 0).reshape(P, 48 * 4)
    cb = inputs["a_conv_b"][0].reshape(48, P).T
    Dc = np.repeat(inputs["a_D"][0], HD).reshape(32, P).T
    ng = inputs["a_norm_g"][0].reshape(32, P).T
    return {
        "x": f(inputs["x"][b]),
        "cT": f(inputs["c"][b].reshape(KC, P).T),
        "ada_w": f(inputs["ada_w"]),
        "ada_b": f(inputs["ada_b"].reshape(1, 12288)),
        "ln_g": f(inputs["ln_g"]),
        "ln_b": f(inputs["ln_b"]),
        "a_in_w": f(inputs["a_in_w"][0]),
        "cw": f(cwl),
        "cb": f(cb),
        "dtb": f(inputs["a_dt_bias"].reshape(1, NH)),
        "alog": f(inputs["a_A_log"].reshape(1, NH)),
        "Dc": f(Dc),
        "ng": f(ng),
        "a_out_w": f(inputs["a_out_w"][0]),
        "kv_w": f(inputs["kv_w"]),
        "b_in_w": f(inputs["b_in_w"][0]),
        "b_out_w": f(inputs["b_out_w"][0]),
        "cst": cst,
        "abias": ab,
    }


def kernel(**inputs):
    inputs = {k: np.asarray(v) for k, v in inputs.items()}
    cst, ab = make_consts()
    nc, _ = build()
    in_maps = [core_inputs(b, inputs, cst, ab) for b in range(NCORES)]
    res = run_bass_kernel_spmd(nc, in_maps, core_ids=list(range(NCORES)))
    out = np.stack([np.asarray(res.results[b]["out"]) for b in range(NCORES)], axis=0)
    return out.astype(np.float32)
```

```python
import numpy as np
import concourse.bass as bass
import concourse.mybir as mybir
from concourse.bass_utils import run_bass_kernel_spmd
from contextlib import ExitStack

F32 = mybir.dt.float32
BF16 = mybir.dt.bfloat16
AF = mybir.ActivationFunctionType
ALU = mybir.AluOpType
AX = mybir.AxisListType

ENGINES = ("pe", "act", "dve", "pool", "sp")


class Buf:
    __slots__ = ("name", "last_write", "readers", "dma_n", "sem", "exclusive")

    def __init__(self, name, exclusive=False):
        self.name = name
        self.exclusive = exclusive
        self.last_write = None
        self.readers = []
        self.dma_n = 0
        self.sem = None


class Op:
    __slots__ = ("eng", "fn", "waits", "idx", "is_dma", "dst", "dma_val", "signaled",
                 "semval", "ninc")


class Sched:
    SAME_ENGINE_SYNC = True

    def __init__(self, nc):
        self.nc = nc
        self.q = {e: [] for e in ENGINES}
        self.known = {e: {} for e in ENGINES}
        self.dma_bufs = []
        self.final_bufs = []

    def _event(self, op):
        if op.is_dma:
            return (op.dst, op.dma_val)
        return (op.eng, op.idx)

    def op(self, eng, fn, reads=(), writes=(), dma=False, ninc=1, nowaw=False):
        o = Op()
        o.eng = eng
        o.fn = fn
        o.is_dma = dma
        o.signaled = False
        o.semval = None
        o.ninc = ninc
        o.idx = len(self.q[eng])
        o.dst = None
        o.dma_val = None
        deps = []
        excl = [b for b in reads if b.exclusive and b not in writes]
        if excl:
            assert not dma
            reads = [b for b in reads if not b.exclusive]
            writes = list(writes) + excl
        for b in reads:
            if b.last_write is not None:
                deps.append(b.last_write)
        for b in writes:
            if b.last_write is not None and not nowaw:
                deps.append(b.last_write)
            deps.extend(b.readers)
        if dma:
            assert len(writes) == 1
            o.dst = writes[0]
            if o.dst.dma_n == 0 and o.dst not in self.dma_bufs:
                self.dma_bufs.append(o.dst)
            o.dst.dma_n += ninc
            o.dma_val = o.dst.dma_n
        known = self.known[eng]
        waits = {}
        for d in deps:
            if d is o:
                continue
            key, val = self._event(d)
            if not d.is_dma and d.eng == eng:
                if eng == "pe" or eng == "sp" or not self.SAME_ENGINE_SYNC:
                    continue
            if known.get(key, -1) >= val:
                continue
            if waits.get(key, (-1, None))[0] < val:
                waits[key] = (val, d)
        o.waits = []
        for key, (val, d) in waits.items():
            known[key] = val
            d.signaled = True
            o.waits.append(d)
        for b in reads:
            b.readers.append(o)
        for b in writes:
            b.last_write = o
            if not nowaw:
                b.readers = []
        self.q[eng].append(o)
        return o

    def finish(self, out_bufs):
        self.final_bufs = list(out_bufs)

    def emit(self, stack):
        nc = self.nc
        esem = {}
        for e in ENGINES:
            esem[e] = stack.enter_context(nc.semaphore(f"s_{e}"))
        for i, b in enumerate(self.dma_bufs):
            b.sem = stack.enter_context(nc.semaphore(f"d{i}"))
        for e in ENGINES:
            c = 0
            for o in self.q[e]:
                if not o.is_dma and o.signaled:
                    c += 1
                    o.semval = c
        block = stack.enter_context(nc.Block())
        final_bufs = self.final_bufs

        def run(engine, name):
            for o in self.q[name]:
                for d in o.waits:
                    if d.is_dma:
                        engine.wait_ge(d.dst.sem, 16 * d.dma_val)
                    else:
                        engine.wait_ge(esem[d.eng], d.semval)
                inst = o.fn(engine)
                if o.is_dma:
                    if isinstance(inst, (list, tuple)):
                        assert len(inst) == o.ninc
                        for i_ in inst:
                            i_.then_inc(o.dst.sem, 16)
                    else:
                        assert o.ninc == 1
                        inst.then_inc(o.dst.sem, 16)
                elif o.signaled:
                    inst.then_inc(esem[name], 1)
            if name == "sp":
                for b in final_bufs:
                    engine.wait_ge(b.sem, 16 * b.dma_n)

        @block.sync
        def _(e):
            run(e, "sp")

        @block.scalar
        def _(e):
            run(e, "act")

        @block.vector
        def _(e):
            run(e, "dve")

        @block.gpsimd
        def _(e):
            run(e, "pool")

        @block.tensor
        def _(e):
            run(e, "pe")

    def stats(self):
        return {e: (len(self.q[e]), sum(len(o.waits) for o in self.q[e])) for e in ENGINES}


class Tile:
    __slots__ = ("ap", "buf")

    def __init__(self, ap, buf):
        self.ap = ap
        self.buf = buf

    def __getitem__(self, k):
        return self.ap[k]


class Arena:
    def __init__(self, tensor, nwords, name):
        self.t = tensor
        self.cap = nwords
        self.top = 0
        self.live = []
        self.name = name
        self.peak = 0

    def mark(self):
        return self.top

    def release(self, m):
        self.top = m

    def _mk(self, name, start, words):
        end = start + words
        assert end <= self.cap, f"{self.name} arena overflow: {name} needs {end} > {self.cap}"
        self.peak = max(self.peak, end)
        b = Buf(name)
        keep = []
        for (s, e, ob) in self.live:
            if s < end and e > start:
                if ob.last_write is not None:
                    b.readers.append(ob.last_write)
                b.readers.extend(ob.readers)
                if s >= start and e <= end:
                    continue
            keep.append((s, e, ob))
        keep.append((start, end, b))
        self.live = keep
        return b

    def alloc(self, name, nelem, dt=F32):
        size = 4 if dt == F32 else 2
        words = (nelem * size + 3) // 4
        words = (words + 7) // 8 * 8
        start = self.top
        self.top += words
        b = self._mk(name, start, words)
        v = self.t[:, start:start + words]
        if dt != F32:
            v = v.bitcast(dt)
        v = v[:, 0:nelem]
        return Tile(v, b)

    def at(self, name, start, nelem, dt=F32):
        size = 4 if dt == F32 else 2
        words = (nelem * size + 3) // 4
        if self.name == "psum":
            bank = start // 512
            assert (start + words - 1) // 512 == bank, "psum tile straddles banks"
            if not hasattr(self, "banks"):
                self.banks = [Buf(f"psbank{i}", exclusive=True) for i in range(8)]
            b = self.banks[bank]
        else:
            b = self._mk(name, start, words)
        v = self.t[:, start:start + words]
        if dt != F32:
            v = v.bitcast(dt)
        v = v[:, 0:nelem]
        return Tile(v, b)


P = 128
D = 2048
KC = 16
T = 4096
SEG = 1024
NSEG = T // SEG
CH = 128
NCH = SEG // CH
DI = 4096
NH = 64
NG = 8
DS = 128
HD = 64
INW = 10304
ALPHA = (2 * 2) ** 0.25
LN_EPS = 1e-5
RMS_EPS = 1e-5
NEG = -30000.0
DIL = ((128, 1), (512, 4), (2048, 16))
NCORES = 4
E_DEPTH = 2
E_STAG = 16
D_DEPTH = 4
D_STAG = 5

SBUF_WORDS = 53000


def build(stop=None):
    nc = bass.Bass("TRN2", target_bir_lowering=False)
    S = Sched(nc)

    def din(name, shape, dt=F32):
        return nc.dram_tensor(name, list(shape), dt, kind="ExternalInput").ap()

    def dscr(name, shape, dt=F32):
        return nc.dram_tensor(name, list(shape), dt, kind="Internal").ap()

    x_d = din("x", [T, D])
    cT_d = din("cT", [P, KC])
    adaw_d = din("ada_w", [2, D, 6144])
    adab_d = din("ada_b", [1, 12288])
    lng_d = din("ln_g", [2, D])
    lnb_d = din("ln_b", [2, D])
    inw_d = din("a_in_w", [D, INW])
    cw_d = din("cw", [P, 48 * 4])
    cb_d = din("cb", [P, 48])
    dtb_d = din("dtb", [1, NH])
    alog_d = din("alog", [1, NH])
    Dc_d = din("Dc", [P, 32])
    ng_d = din("ng", [P, 32])
    ow_d = din("a_out_w", [DI, D])
    kvw_d = din("kv_w", [D, 6144])
    binw_d = din("b_in_w", [D, 4096])
    bow_d = din("b_out_w", [1024, D])
    cst_d = din("cst", [P, 384])
    abias_d = din("abias", [P, 24 * 256])
    out_d = nc.dram_tensor("out", [T, D], F32, kind="ExternalOutput").ap()

    mod_s = dscr("mod_s", [1, 12288])
    yn_s = dscr("yn_s", [T // P, P, 32, P], BF16)
    st_s = dscr("st_s", [NG, P, 512])
    halo_s = dscr("halo_s", [48, P, 4])
    x1_s = dscr("x1_s", [T, D])
    x1T_s = dscr("x1T_s", [KC, P, T], BF16)
    h1T_s = dscr("h1T_s", [KC, P, T], BF16)
    KT_s = dscr("KT_s", [24, P, T], BF16)
    QT_s = dscr("QT_s", [24, P, T], BF16)
    V_s = dscr("V_s", [3, T, 1024], BF16)
    O_s = dscr("O_s", [3, T, 1024])
    L_s = dscr("L_s", [3, T, 8])

    dbg = {}
    dbgd = {}

    with ExitStack() as st:
        sb_t = st.enter_context(nc.sbuf_tensor("arena", [P, SBUF_WORDS], F32))
        ps_t = st.enter_context(nc.psum_tensor("psall", [P, 4096], F32))
        A = Arena(sb_t, SBUF_WORDS, "sbuf")
        PS = Arena(ps_t, 4096, "psum")

        def OP(eng, fn, r=(), w=(), **kw):
            return S.op(eng, fn, reads=[t.buf for t in r], writes=[t.buf for t in w], **kw)

        def DMA(eng, out_ap, in_ap, r=(), w=(), **kw):
            return S.op(eng, lambda e, o=out_ap, i=in_ap: e.dma_start(out=o, in_=i),
                        reads=[t.buf for t in r], writes=[t.buf for t in w], dma=True, **kw)

        def MM(out_t, out_ap, l_t, l_ap, r_t, r_ap, start, stop, extra_r=()):
            return S.op("pe", lambda e, o=out_ap, l=l_ap, r=r_ap, s_=start, p_=stop:
                        e.matmul(o, lhsT=l, rhs=r, start=s_, stop=p_),
                        reads=[l_t.buf, r_t.buf] + [t.buf for t in extra_r], writes=[out_t.buf])

        def TR(out_t, out_ap, in_t, in_ap, id_t, id_ap):
            return S.op("pe", lambda e, o=out_ap, i=in_ap, d=id_ap: e.transpose(o, i, d),
                        reads=[in_t.buf, id_t.buf], writes=[out_t.buf])

        def ACT(out_t, out_ap, in_t, in_ap, func, bias=None, scale=None, r=(), accum=None):
            kw = {}
            if bias is not None:
                kw["bias"] = bias
            if scale is not None:
                kw["scale"] = scale
            if accum is not None:
                kw["accum_out"] = accum
            return S.op("act", lambda e, o=out_ap, i=in_ap, f=func, kw=kw: e.activation(out=o, in_=i, func=f, **kw),
                        reads=[in_t.buf] + [t.buf for t in r], writes=[out_t.buf])

        def TT(eng, out_t, out_ap, a_t, a_ap, b_t, b_ap, op):
            return S.op(eng, lambda e, o=out_ap, a=a_ap, b=b_ap, op=op: e.tensor_tensor(out=o, in0=a, in1=b, op=op),
                        reads=[a_t.buf, b_t.buf], writes=[out_t.buf])

        def TS(eng, out_t, out_ap, a_t, a_ap, s1, s2, op0, op1=None, r=()):
            def f(e, o=out_ap, a=a_ap, s1=s1, s2=s2, op0=op0, op1=op1):
                if op1 is None:
                    return e.tensor_scalar(out=o, in0=a, scalar1=s1, scalar2=None, op0=op0)
                return e.tensor_scalar(out=o, in0=a, scalar1=s1, scalar2=s2, op0=op0, op1=op1)
            return S.op(eng, f, reads=[a_t.buf] + [t.buf for t in r], writes=[out_t.buf])

        def STT(out_t, out_ap, a_t, a_ap, scalar, b_t, b_ap, op0, op1, r=()):
            return S.op("dve", lambda e, o=out_ap, a=a_ap, s_=scalar, b=b_ap, op0=op0, op1=op1:
                        e.scalar_tensor_tensor(out=o, in0=a, scalar=s_, in1=b, op0=op0, op1=op1),
                        reads=[a_t.buf, b_t.buf] + [t.buf for t in r], writes=[out_t.buf])

        def CP(eng, out_t, out_ap, in_t, in_ap):
            if eng == "act":
                return S.op("act", lambda e, o=out_ap, i=in_ap: e.copy(out=o, in_=i), reads=[in_t.buf], writes=[out_t.buf])
            return S.op(eng, lambda e, o=out_ap, i=in_ap: e.tensor_copy(out=o, in_=i), reads=[in_t.buf], writes=[out_t.buf])

        def MEMSET(eng, t, ap, val):
            return S.op(eng, lambda e, a=ap, v=val: e.memset(a, v), writes=[t])

        B_mod = Buf("mod_s")
        B_yn = [Buf(f"yn_s{i}") for i in range(NSEG)]
        B_st = [Buf(f"st_s{i}") for i in range(NG)]
        B_halo = Buf("halo")
        B_x1 = Buf("x1_s")
        B_x1T = Buf("x1T_s")
        B_h1T = Buf("h1T_s")
        B_out = Buf("out")
        B_KT = Buf("KT_s")
        B_QT = Buf("QT_s")
        B_V = Buf("V_s")
        B_O = Buf("O_s")
        B_L = Buf("L_s")

        def DT(b):
            return Tile(None, b)

        cst = A.alloc("cst", 384)
        DMA("sp", cst.ap, cst_d, w=[cst])
        ident = cst.ap[:, 0:128]
        triU = cst.ap[:, 128:256]
        maskb = cst.ap[:, 256:384]
        ones_f = A.alloc("ones_f", 128)
        MEMSET("dve", ones_f.buf, ones_f.ap, 1.0)
        ones_b = A.alloc("ones_b", 128, BF16)
        MEMSET("dve", ones_b.buf, ones_b.ap, 1.0)
        modF = A.alloc("modF", 64)
        base_mark = A.mark()

        m0 = A.mark()
        cT = A.alloc("cT", KC)
        DMA("sp", cT.ap, cT_d, w=[cT])
        scb = A.alloc("scb", KC, BF16)
        ACT(scb, scb.ap, cT, cT.ap, AF.Silu)
        adab = A.alloc("adab", 12288)
        DMA("sp", adab.ap[0:1, :], adab_d, w=[adab])
        modrow = A.alloc("modrow", 12288)
        wring = [A.alloc(f"adaw{i}", KC * 512, BF16) for i in range(3)]
        psrow = [PS.at(f"psrow{i}", i * 512, 512) for i in range(2)]
        it = 0
        p0_issued = [0]

        def p0_issue(upto):
            while p0_issued[0] < min(upto, 24):
                i_ = p0_issued[0]
                p0_issued[0] += 1
                l_, nb_ = divmod(i_, 12)
                wt_ = wring[i_ % 3]
                DMA("pool", wt_.ap.rearrange("p (k n) -> p k n", k=KC),
                    adaw_d[l_].rearrange("(k p) n -> p k n", p=P)[:, :, nb_ * 512:(nb_ + 1) * 512], w=[wt_])
        for l in range(2):
            for nb in range(12):
                p0_issue(it + 3)
                wt = wring[it % 3]
                pr = psrow[it % 2]
                w3 = wt.ap.rearrange("p (k n) -> p k n", k=KC)
                for k in range(KC):
                    MM(pr, pr.ap[0:1, :], scb, scb.ap[:, k:k + 1], wt, w3[:, k, :], k == 0, k == KC - 1)
                c0 = l * 6144 + nb * 512
                TT("dve", modrow, modrow.ap[0:1, c0:c0 + 512], pr, pr.ap[0:1, :], adab, adab.ap[0:1, c0:c0 + 512], ALU.add)
                it += 1
        DMA("sp", mod_s, modrow.ap[0:1, :], r=[modrow], w=[DT(B_mod)])
        psF = PS.at("psF", 1024, 64)
        for l in range(2):
            for j in range(32):
                c0 = l * 6144 + j * 128
                MM(psF, psF.ap[:, l * 32 + j:l * 32 + j + 1], modrow, modrow.ap[0:1, c0:c0 + 128],
                   ones_f, ones_f.ap[0:1, 0:1], True, True)
        CP("dve", modF, modF.ap, psF, psF.ap)
        for l in range(2):
            TS("dve", modF, modF.ap[:, l * 32 + 16:l * 32 + 32], modF, modF.ap[:, l * 32 + 16:l * 32 + 32], 1.0, None, ALU.add)
        A.release(m0)
        if stop == "p0":
            dbg["modF"] = (modF, [P, 64])


        if stop != "p0":
            mA = A.mark()
            cw = A.alloc("cw", 192)
            DMA("sp", cw.ap, cw_d, w=[cw])
            cb = A.alloc("cb", 48)
            DMA("sp", cb.ap, cb_d, w=[cb])
            Dc = A.alloc("Dc", 32)
            DMA("sp", Dc.ap, Dc_d, w=[Dc])
            ng = A.alloc("ng", 32)
            DMA("sp", ng.ap, ng_d, w=[ng])
            dtb = A.alloc("dtb", NH)
            DMA("sp", dtb.ap, dtb_d.partition_broadcast(P), w=[dtb])
            Abc = A.alloc("Abc", NH)
            DMA("sp", Abc.ap, alog_d.partition_broadcast(P), w=[Abc])
            ACT(Abc, Abc.ap, Abc, Abc.ap, AF.Exp)
            TS("dve", Abc, Abc.ap, Abc, Abc.ap, -1.0, None, ALU.mult)
            inw3 = inw_d.rearrange("(k p) n -> p k n", p=P)
            wdt = A.alloc("wdt", KC * NH, BF16)
            wdt3 = wdt.ap.rearrange("p (k n) -> p k n", k=KC)
            DMA("pool", wdt3, inw3[:, :, 10240:10304], w=[wdt])
            hT = A.alloc("hT", KC * SEG, BF16)
            hT3 = hT.ap.rearrange("p (k t) -> p k t", k=KC)
            xt = A.alloc("xt", D)
            wq = [A.alloc(f"wq{i}", KC * 256, BF16) for i in range(3)]
            raw = [A.alloc(f"raw{i}", 3 + SEG) for i in range(2)]
            ctmp = A.alloc("ctmp", SEG)
            sgc = A.alloc("sgc", SEG)
            sgz = [A.alloc(f"sgz{i}", 512) for i in range(2)]
            cnt_z = 0
            sets = [dict(xs=A.alloc(f"xs{i}", 4 * SEG), Bs=A.alloc(f"Bs{i}", SEG), BT=A.alloc(f"BT{i}", SEG, BF16),
                         CT=A.alloc(f"CT{i}", SEG, BF16)) for i in range(2)]
            zs = A.alloc("zs", 4 * SEG, BF16)
            zs3 = zs.ap.rearrange("p (i t) -> p i t", i=4)
            dt_t = A.alloc("dt_t", NCH * NH)
            dtt_t = A.alloc("dtt_t", NCH * NH)
            cum_t = A.alloc("cum_t", NCH * NH)
            eTot = A.alloc("eTot", NCH * NH)
            dt3 = dt_t.ap.rearrange("p (c h) -> p c h", c=NCH)
            dtt3 = dtt_t.ap.rearrange("p (c h) -> p c h", c=NCH)
            cum3 = cum_t.ap.rearrange("p (c h) -> p c h", c=NCH)
            eT3 = eTot.ap.rearrange("p (c h) -> p c h", c=NCH)
            cumT = A.alloc("cumT", SEG)
            St = A.alloc("St", 512)
            Sbf = A.alloc("Sbf", 512, BF16)
            ynT = A.alloc("ynT", 4 * SEG, BF16)
            ynT4 = ynT.ap.rearrange("p (c i t) -> p c i t", c=NCH, i=4)
            halo = A.alloc("halo", 48 * 4)
            halo3 = halo.ap.rearrange("p (j k) -> p j k", j=48)
            MEMSET("pool", halo.buf, halo.ap, 0.0)
            segt = A.alloc("segt", 512)
            Ebc = A.alloc("Ebc", 512)
            pre = [A.alloc(f"pre{i}", NCH * NH) for i in range(4)]
            tmps = []
            for i in range(2):
                tmps.append(dict(
                    xdt=A.alloc(f"xdt{i}", 512, BF16), xtl=A.alloc(f"xtl{i}", 512, BF16),
                    Btm=A.alloc(f"Btm{i}", 128, BF16), CBT=A.alloc(f"CBT{i}", 128),
                    MT=A.alloc(f"MT{i}", 1024, BF16), CkT=A.alloc(f"CkT{i}", 1024, BF16),
                    yg=A.alloc(f"yg{i}", 512), ysq=A.alloc(f"ysq{i}", 512, BF16),
                    rstd=A.alloc(f"rstd{i}", 128), lnv=A.alloc(f"lnv{i}", 128)))

            def v3(ap, a):
                return ap.rearrange("p (a b) -> p a b", a=a)

            def rr(*gens):
                gens = list(gens)
                while gens:
                    for g_ in list(gens):
                        try:
                            next(g_)
                            yield
                        except StopIteration:
                            gens.remove(g_)

            def drain(gen):
                for _ in gen:
                    pass

            cnt_i = 0
            cnt_r = 0
            cnt_w = 0
            cnt_s = 0
            cnt_c = 0
            def xbc_req(g):
                return [(DI + g * 512, 256), (DI + g * 512 + 256, 256), (2 * DI + g * P, P), (2 * DI + 1024 + g * P, P)]
            wreq = []
            for seg_ in range(NSEG):
                wreq += xbc_req(0)
                for g_ in range(NG):
                    wreq += [(g_ * 512, 256), (g_ * 512 + 256, 256)]
                    if g_ + 1 < NG:
                        wreq += xbc_req(g_ + 1)
            wstate = [0]
            for seg in range(NSEG):
                t0 = seg * SEG
                def gen_hT(seg_):
                    tt0_ = seg_ * SEG
                    psT = [PS.at(f"psT{i}", i * 512, 512) for i in range(2)]
                    for j in range(SEG // P):
                        DMA("sp", xt.ap, x_d[tt0_ + j * P:tt0_ + (j + 1) * P, :], w=[xt])
                        for q4 in range(4):
                            pt = psT[(j * 4 + q4) % 2]
                            for i in range(4):
                                k = q4 * 4 + i
                                TR(pt, pt.ap[:, i * P:(i + 1) * P], xt, xt.ap[:, k * P:(k + 1) * P], cst, ident)
                            for i in range(4):
                                k = q4 * 4 + i
                                ACT(hT, hT3[:, k, j * P:(j + 1) * P], pt, pt.ap[:, i * P:(i + 1) * P], AF.Identity,
                                    bias=modF.ap[:, k:k + 1], scale=modF.ap[:, 16 + k:17 + k], r=[modF])
                            yield
                if seg == 0:
                    for _ in gen_hT(0):
                        pass
                psDall = PS.at("psDall", 1536, 512)
                psCum = PS.at("psCum", 3072, 512)
                psTot = PS.at("psTot", 3584, 512)
                psCTa = [PS.at(f"psCTa{i}", 2048 + i * 512, 512) for i in range(2)]
                x1, ax, ex, dA = pre
                dA3 = dA.ap.rearrange("p (c h) -> p c h", c=NCH)
                for c in range(NCH):
                    tsl = slice(c * CH, (c + 1) * CH)
                    for k in range(KC):
                        MM(psDall, psDall.ap[:, c * NH:(c + 1) * NH], hT, hT3[:, k, tsl], wdt, wdt3[:, k, :], k == 0, k == KC - 1)
                TT("dve", x1, x1.ap.rearrange("p (c h) -> p c h", c=NCH), psDall, psDall.ap.rearrange("p (c h) -> p c h", c=NCH),
                   dtb, dtb.ap.unsqueeze(1).to_broadcast([P, NCH, NH]), ALU.add)
                ACT(ax, ax.ap, x1, x1.ap, AF.Abs)
                ACT(ex, ex.ap, ax, ax.ap, AF.Exp, scale=-1.0)
                ACT(ex, ex.ap, ex, ex.ap, AF.Ln, bias=1.0, scale=1.0)
                STT(dt_t, dt_t.ap, x1, x1.ap, 0.0, ex, ex.ap, ALU.max, ALU.add)
                TT("dve", dA, dA3, dt_t, dt3, Abc, Abc.ap.unsqueeze(1).to_broadcast([P, NCH, NH]), ALU.mult)
                for c in range(NCH):
                    MM(psCum, psCum.ap[:, c * NH:(c + 1) * NH], cst, triU, dA, dA3[:, c, :], True, True)
                for c in range(NCH):
                    pct = psCTa[c // 4]
                    MM(pct, pct.ap[0:NH, (c % 4) * P:(c % 4 + 1) * P], dA, dA3[:, c, :], cst, triU, True, True)
                for c in range(NCH):
                    MM(psTot, psTot.ap[:, c * NH:(c + 1) * NH], ones_f, ones_f.ap, dA, dA3[:, c, :], True, True)
                CP("act", cum_t, cum_t.ap, psCum, psCum.ap)
                for i in range(2):
                    CP("act", cumT, cumT.ap[0:NH, i * 512:(i + 1) * 512], psCTa[i], psCTa[i].ap[0:NH, :])
                TT("dve", ax, ax.ap, psTot, psTot.ap, cum_t, cum_t.ap, ALU.subtract)
                ACT(ax, ax.ap, ax, ax.ap, AF.Exp)
                TT("dve", dtt_t, dtt_t.ap, ax, ax.ap, dt_t, dt_t.ap, ALU.mult)
                ACT(eTot, eTot.ap, psTot, psTot.ap, AF.Exp)
                psI = [PS.at(f"psI{i}", i * 512, 512) for i in range(2)]
                psX = PS.at("psX", 1024, 512)
                psB = PS.at("psB", 1536, 128)
                psCB = PS.at("psCB", 1536 + 128, 128)
                psBC = PS.at("psBC", 2048, 512)
                ss = PS.at("ss", 2560, 128)
                psY = PS.at("psY", 3072, 512)
                psS = PS.at("psS", 3584, 512)
                def conv_tile(rw, jch, dsts):
                    TS("dve", ctmp, ctmp.ap, rw, rw.ap[:, 0:SEG], cw.ap[:, jch * 4:jch * 4 + 1], cb.ap[:, jch:jch + 1],
                       ALU.mult, ALU.add, r=[cw, cb])
                    yield
                    for kk in range(1, 4):
                        STT(ctmp, ctmp.ap, rw, rw.ap[:, kk:kk + SEG], cw.ap[:, jch * 4 + kk:jch * 4 + kk + 1],
                            ctmp, ctmp.ap, ALU.mult, ALU.add, r=[cw])
                        yield
                    CP("pool", halo, halo3[:, jch, 0:3], rw, rw.ap[:, SEG:SEG + 3])
                    ACT(sgc, sgc.ap, ctmp, ctmp.ap, AF.Exp, scale=-1.0)
                    yield
                    ACT(sgc, sgc.ap, sgc, sgc.ap, AF.Ln, bias=1.0, scale=1.0)
                    yield
                    ACT(sgc, sgc.ap, sgc, sgc.ap, AF.Exp, scale=-1.0)
                    yield
                    for (dt_, dap) in dsts:
                        TT("pool", dt_, dap, ctmp, ctmp.ap, sgc, sgc.ap, ALU.mult)
                    yield

                def proj_cols(wt, w3, coff, evac):
                    nonlocal cnt_i
                    for blk in range(SEG // 512):
                        ps = psI[cnt_i % 2]
                        cnt_i += 1
                        for k in range(KC):
                            MM(ps, ps.ap, wt, w3[:, k, coff:coff + P], hT, hT3[:, k, blk * 512:(blk + 1) * 512],
                               k == 0, k == KC - 1)
                        evac(ps, blk)
                        yield

                def silu_evac(dst_t, dst_ap, ps):
                    nonlocal cnt_z
                    sg = sgz[cnt_z % 2]
                    cnt_z += 1
                    ACT(sg, sg.ap, ps, ps.ap, AF.Exp, scale=-1.0)
                    ACT(sg, sg.ap, sg, sg.ap, AF.Ln, bias=1.0, scale=1.0)
                    ACT(sg, sg.ap, sg, sg.ap, AF.Exp, scale=-1.0)
                    TT("dve", dst_t, dst_ap, ps, ps.ap, sg, sg.ap, ALU.mult)

                def issue_w(i):
                    if i >= len(wreq) or i < wstate[0]:
                        return
                    assert i == wstate[0]
                    wstate[0] += 1
                    c0_, nc_ = wreq[i]
                    wt_ = wq[i % len(wq)]
                    w3_ = wt_.ap.rearrange("p (k n) -> p k n", k=KC)
                    DMA("pool", w3_[:, :, 0:nc_], inw3[:, :, c0_:c0_ + nc_], w=[wt_])

                def load_w(c0, ncols):
                    nonlocal cnt_w
                    i = cnt_w
                    cnt_w += 1
                    assert wreq[i] == (c0, ncols), (i, wreq[i], c0, ncols)
                    for j_ in range(wstate[0], i + len(wq)):
                        issue_w(j_)
                    wt = wq[i % len(wq)]
                    w3 = wt.ap.rearrange("p (k n) -> p k n", k=KC)
                    return wt, w3

                def gen_xbc(g, ss_):
                    nonlocal cnt_r
                    xs_, Bs_, BT_, CT_ = ss_["xs"], ss_["Bs"], ss_["BT"], ss_["CT"]
                    xs3_ = xs_.ap.rearrange("p (i t) -> p i t", i=4)
                    for hp in range(2):
                        wt, w3 = load_w(DI + g * 512 + hp * 256, 256)
                        for i2 in range(2):
                            i = hp * 2 + i2
                            rw = raw[cnt_r % 2]
                            cnt_r += 1
                            jch = g * 4 + i
                            CP("pool", rw, rw.ap[:, 0:3], halo, halo3[:, jch, 0:3])
                            yield from proj_cols(wt, w3, i2 * P, lambda ps, blk, rw=rw: CP("act", rw, rw.ap[:, 3 + blk * 512:3 + (blk + 1) * 512], ps, ps.ap))
                            yield from conv_tile(rw, jch, [(xs_, xs3_[:, i, :])])
                    for which in range(2):
                        wt, w3 = load_w(DI + DI + which * 1024 + g * P, P)
                        rw = raw[cnt_r % 2]
                        cnt_r += 1
                        jch = 32 + which * 8 + g
                        CP("pool", rw, rw.ap[:, 0:3], halo, halo3[:, jch, 0:3])
                        yield from proj_cols(wt, w3, 0, lambda ps, blk, rw=rw: CP("act", rw, rw.ap[:, 3 + blk * 512:3 + (blk + 1) * 512], ps, ps.ap))
                        if which == 0:
                            yield from conv_tile(rw, jch, [(Bs_, Bs_.ap), (BT_, BT_.ap)])
                        else:
                            yield from conv_tile(rw, jch, [(CT_, CT_.ap)])

                def gen_z(g):
                    for hp in range(2):
                        wt, w3 = load_w(g * 512 + hp * 256, 256)
                        for i2 in range(2):
                            i = hp * 2 + i2
                            yield from proj_cols(wt, w3, i2 * P, lambda ps, blk, i=i: silu_evac(zs, zs3[:, i, blk * 512:(blk + 1) * 512], ps))

                def gen_prep(g, c, ss_, tm):
                    xs_, Bs_, BT_, CT_ = ss_["xs"], ss_["Bs"], ss_["BT"], ss_["CT"]
                    xs3_ = xs_.ap.rearrange("p (i t) -> p i t", i=4)
                    tsl = slice(c * CH, (c + 1) * CH)
                    xdt, xtl, Btm, CBT, MT, CkT = (tm[k_] for k_ in ("xdt", "xtl", "Btm", "CBT", "MT", "CkT"))
                    hs = slice(g * 8, (g + 1) * 8)
                    for i in range(4):
                        TR(psX, psX.ap[:, i * P:(i + 1) * P], xs_, xs3_[:, i, tsl], cst, ident)
                    TR(psB, psB.ap, Bs_, Bs_.ap[:, tsl], cst, ident)
                    MM(psCB, psCB.ap, BT_, BT_.ap[:, tsl], CT_, CT_.ap[:, tsl], True, True)
                    yield
                    TT("dve", xdt, v3(xdt.ap, 8), psX, v3(psX.ap, 8), dt_t, dt3[:, c, hs].unsqueeze(2).to_broadcast([P, 8, HD]), ALU.mult)
                    CP("act", Btm, Btm.ap, psB, psB.ap)
                    CP("act", CBT, CBT.ap, psCB, psCB.ap)
                    yield
                    TT("dve", xtl, v3(xtl.ap, 8), psX, v3(psX.ap, 8), dtt_t, dtt3[:, c, hs].unsqueeze(2).to_broadcast([P, 8, HD]), ALU.mult)
                    yield
                    for hh in range(2):
                        for jj in range(4):
                            head = g * 8 + hh * 4 + jj
                            MM(psBC, psBC.ap[:, jj * P:(jj + 1) * P], cst, ident[0:NH, head:head + 1].to_broadcast([NH, P]),
                               cumT, cumT.ap[0:NH, tsl], True, True)
                        yield
                        h0_ = g * 8 + hh * 4
                        TT("dve", segt, v3(segt.ap, 4), psBC, v3(psBC.ap, 4),
                           cum_t, cum3[:, c, h0_:h0_ + 4].unsqueeze(2).to_broadcast([P, 4, P]), ALU.subtract)
                        ACT(Ebc, Ebc.ap, psBC, psBC.ap, AF.Exp)
                        yield
                        TT("dve", segt, v3(segt.ap, 4), segt, v3(segt.ap, 4),
                           cst, maskb.unsqueeze(1).to_broadcast([P, 4, P]), ALU.add)
                        yield
                        ACT(segt, segt.ap, segt, segt.ap, AF.Exp)
                        TT("pool", CkT, v3(CkT.ap[:, hh * 512:(hh + 1) * 512], 4), Ebc, v3(Ebc.ap, 4),
                           CT_, CT_.ap[:, tsl].unsqueeze(1).to_broadcast([P, 4, P]), ALU.mult)
                        yield
                        TT("dve", MT, v3(MT.ap[:, hh * 512:(hh + 1) * 512], 4), segt, v3(segt.ap, 4),
                           CBT, CBT.ap.unsqueeze(1).to_broadcast([P, 4, P]), ALU.mult)
                        yield

                def gen_tail(g, c, ss_, tm):
                    xs_ = ss_["xs"]
                    xs3_ = xs_.ap.rearrange("p (i t) -> p i t", i=4)
                    tsl = slice(c * CH, (c + 1) * CH)
                    xdt, xtl, Btm, CBT, MT, CkT, yg, ysq, rstd, lnv = (tm[k_] for k_ in
                        ("xdt", "xtl", "Btm", "CBT", "MT", "CkT", "yg", "ysq", "rstd", "lnv"))
                    hs = slice(g * 8, (g + 1) * 8)
                    St3 = v3(St.ap, 8)
                    yg3 = v3(yg.ap, 4)
                    for j in range(8):
                        i, half = j // 2, j % 2
                        yo = psY.ap[half * HD:(half + 1) * HD, i * P:(i + 1) * P]
                        MM(psY, yo, xdt, xdt.ap[:, j * HD:(j + 1) * HD], MT, MT.ap[:, j * P:(j + 1) * P], True, False)
                        MM(psY, yo, Sbf, Sbf.ap[:, j * HD:(j + 1) * HD], CkT, CkT.ap[:, j * P:(j + 1) * P], False, True)
                    MM(psS, psS.ap, Btm, Btm.ap, xtl, xtl.ap, True, True)
                    TT("pool", yg, yg3, xs_, xs3_[:, :, tsl], Dc, Dc.ap[:, g * 4:(g + 1) * 4].unsqueeze(2).to_broadcast([P, 4, P]), ALU.mult)
                    yield
                    TT("dve", St, St3, St, St3, eTot, eT3[:, c, hs].unsqueeze(2).to_broadcast([P, 8, HD]), ALU.mult)
                    yield
                    TT("dve", St, St.ap, psS, psS.ap, St, St.ap, ALU.add)
                    yield
                    CP("pool", Sbf, Sbf.ap, St, St.ap)
                    TT("dve", yg, yg3, psY, v3(psY.ap, 4), yg, yg3, ALU.add)
                    yield
                    TT("dve", yg, yg3, yg, yg3, zs, zs3[:, :, tsl], ALU.mult)
                    yield
                    ACT(ysq, ysq.ap, yg, yg.ap, AF.Square)
                    yield
                    for i in range(4):
                        MM(ss, ss.ap, ones_b, ones_b.ap, ysq, ysq.ap[:, i * P:(i + 1) * P], i == 0, i == 3)
                    yield
                    ACT(lnv, lnv.ap, ss, ss.ap, AF.Ln, bias=RMS_EPS, scale=1.0 / 512)
                    yield
                    ACT(rstd, rstd.ap, lnv, lnv.ap, AF.Exp, scale=-0.5)
                    yield
                    TT("dve", yg, yg3, yg, yg3, rstd, rstd.ap.unsqueeze(1).to_broadcast([P, 4, P]), ALU.mult)
                    yield
                    TT("pool", ynT, ynT4[:, c], yg, yg3, ng, ng.ap[:, g * 4:(g + 1) * 4].unsqueeze(2).to_broadcast([P, 4, P]), ALU.mult)
                    yield

                def gen_scan(g, ss_):
                    nonlocal cnt_c
                    if seg == 0:
                        MEMSET("pool", St.buf, St.ap, 0.0)
                    else:
                        DMA("sp", St.ap, st_s[g], r=[DT(B_st[g])], w=[St])
                    CP("pool", Sbf, Sbf.ap, St, St.ap)
                    tms = []
                    for c in range(NCH):
                        tms.append(tmps[cnt_c % 2])
                        cnt_c += 1
                    yield from gen_prep(g, 0, ss_, tms[0])
                    for c in range(NCH):
                        gens = [gen_tail(g, c, ss_, tms[c])]
                        if c + 1 < NCH:
                            gens.append(gen_prep(g, c + 1, ss_, tms[c + 1]))
                        yield from rr(*gens)
                    DMA("act", yn_s[seg * NCH:(seg + 1) * NCH, :, g * 4:(g + 1) * 4, :].rearrange("c p i t -> p c i t"), ynT4, r=[ynT], w=[DT(B_yn[seg])], nowaw=True)
                    if seg < NSEG - 1:
                        DMA("act", st_s[g], St.ap, r=[St], w=[DT(B_st[g])])

                def chain(*gens):
                    for g_ in gens:
                        yield from g_

                if seg == 0:
                    drain(gen_xbc(0, sets[0]))
                for g in range(NG):
                    nxt = []
                    if g + 1 < NG:
                        nxt.append(gen_xbc(g + 1, sets[(g + 1) % 2]))
                    elif seg + 1 < NSEG:
                        nxt.append(gen_hT(seg + 1))
                        nxt.append(gen_xbc(0, sets[0]))
                    drain(rr(chain(gen_z(g), *nxt), gen_scan(g, sets[g % 2])))
            if stop == "pA":
                dbgd["yn_s"] = (yn_s.rearrange("c p i t -> c p (i t)"), B_yn, [T // P, P, 32 * P], BF16)
            A.release(mA)


        if stop not in ("p0", "pA"):
            mB = A.mark()
            lng = A.alloc("lng", D)
            lnb = A.alloc("lnb", D)
            g1b0 = A.alloc("g1b0", D)
            DMA("sp", g1b0.ap, mod_s[0:1, 4096:6144].partition_broadcast(P), r=[DT(B_mod)], w=[g1b0])
            TS("pool", g1b0, g1b0.ap, g1b0, g1b0.ap, 1.0, None, ALU.add)
            g1bc = [g1b0, None]
            DMA("sp", lng.ap, lng_d[0:1, :].partition_broadcast(P), w=[lng])
            DMA("sp", lnb.ap, lnb_d[0:1, :].partition_broadcast(P), w=[lnb])
            ow = A.alloc("ow", 32 * D, BF16)
            ow3 = ow.ap.rearrange("p (i n) -> p i n", i=32)
            owd = ow_d.rearrange("(i p) n -> p i n", p=P)
            for q4 in range(4):
                DMA("pool", ow3[:, q4 * 8:(q4 + 1) * 8, :], owd[:, q4 * 8:(q4 + 1) * 8, :], w=[ow], nowaw=True)
            ynt = [A.alloc(f"ynt{i}", 32 * P, BF16) for i in range(2)]
            xtB = [A.alloc(f"xtB{i}", D) for i in range(1)]
            rBs = [A.alloc(f"rB{i}", D) for i in range(2)]
            stg_x = A.alloc("stg_x", KC * P, BF16)
            stg_h = A.alloc("stg_h", KC * P, BF16)
            bst = A.alloc("bst", 24)
            mv = A.alloc("mv", 2)
            rs = A.alloc("rs", 1)
            nmr = A.alloc("nmr", 1)
            psO = [PS.at(f"psO{i}", i * 512, 512) for i in range(4)]
            psTB = [PS.at(f"psTB{i}", 2048 + i * 512, 512) for i in range(4)]
            NT = T // P

            def loadB(j):
                yt = ynt[j % 2]
                DMA("sp", yt.ap, yn_s[j].rearrange("p i t -> p (i t)"),
                    r=[DT(B_yn[(j * P) // SEG])], w=[yt])
                if j == 0:
                    DMA("sp", xtB[0].ap, x_d[0:P, :], w=[xtB[0]])

            def outproj(j):
                yt = ynt[j % 2]
                y3 = yt.ap.rearrange("p (i t) -> p i t", i=32)
                for n in range(4):
                    for i in range(32):
                        MM(psO[n], psO[n].ap, yt, y3[:, i, :], ow, ow3[:, i, n * 512:(n + 1) * 512], i == 0, i == 31)

            def epi1(j):
                rB = rBs[j % 2]
                for n in range(4):
                    TT("dve", rB, rB.ap[:, n * 512:(n + 1) * 512], psO[n], psO[n].ap, g1bc[0], g1bc[0].ap[:, n * 512:(n + 1) * 512], ALU.mult)

            def epi2(j, lidx, dst_d, dst_b, xsrc, do_T):
                rB = rBs[j % 2]
                STT(rB, rB.ap, xsrc, xsrc.ap, ALPHA, rB, rB.ap, ALU.mult, ALU.add)
                if j + 1 < NT:
                    DMA("sp", xtB[0].ap, x_d[(j + 1) * P:(j + 2) * P, :], w=[xtB[0]])
                for n in range(4):
                    S.op("dve", lambda e, o=bst.ap[:, n * 6:(n + 1) * 6], i=rB.ap[:, n * 512:(n + 1) * 512]: e.bn_stats(out=o, in_=i),
                         reads=[rB.buf], writes=[bst.buf])
                S.op("dve", lambda e, o=mv.ap, i=bst.ap: e.bn_aggr(out=o, in_=i), reads=[bst.buf], writes=[mv.buf])
                ACT(rs, rs.ap, mv, mv.ap[:, 1:2], AF.Ln, bias=LN_EPS, scale=1.0)
                ACT(rs, rs.ap, rs, rs.ap, AF.Exp, scale=-0.5)
                TS("dve", nmr, nmr.ap, mv, mv.ap[:, 0:1], rs.ap, -1.0, ALU.mult, ALU.mult, r=[rs])
                ACT(rB, rB.ap, rB, rB.ap, AF.Identity, bias=nmr.ap, scale=rs.ap, r=[nmr, rs])
                TT("dve", rB, rB.ap, rB, rB.ap, lng, lng.ap, ALU.mult)
                TT("dve", rB, rB.ap, rB, rB.ap, lnb, lnb.ap, ALU.add)
                DMA("act", dst_d[j * P:(j + 1) * P, :], rB.ap, r=[rB], w=[DT(dst_b)], nowaw=True)
                if do_T:
                    sx3 = stg_x.ap.rearrange("p (k t) -> p k t", k=KC)
                    sh3 = stg_h.ap.rearrange("p (k t) -> p k t", k=KC)
                    for q4 in range(4):
                        pt = psTB[q4]
                        for i in range(4):
                            k = q4 * 4 + i
                            TR(pt, pt.ap[:, i * P:(i + 1) * P], rB, rB.ap[:, k * P:(k + 1) * P], cst, ident)
                        CP("act", stg_x, stg_x.ap[:, q4 * 512:(q4 + 1) * 512], pt, pt.ap)
                        for i in range(4):
                            k = q4 * 4 + i
                            ACT(stg_h, sh3[:, k, :], pt, pt.ap[:, i * P:(i + 1) * P], AF.Identity,
                                bias=modF.ap[:, 32 + k:32 + k + 1], scale=modF.ap[:, 32 + 16 + k:32 + 17 + k], r=[modF])
                    DMA("act", x1T_s[:, :, j * P:(j + 1) * P].rearrange("k p t -> p k t"), sx3, r=[stg_x], w=[DT(B_x1T)], nowaw=True)
                    DMA("act", h1T_s[:, :, j * P:(j + 1) * P].rearrange("k p t -> p k t"), sh3, r=[stg_h], w=[DT(B_h1T)], nowaw=True)

            loadB(0)
            outproj(0)
            for j in range(NT):
                if j + 1 < NT:
                    loadB(j + 1)
                epi1(j)
                if j + 1 < NT:
                    outproj(j + 1)
                epi2(j, 0, x1_s, B_x1, xtB[0], True)
            if stop == "pB":
                dbgd["x1_s"] = (x1_s.rearrange("(a t) d -> a t d", a=32), [B_x1], [32, P, D], F32)
                dbgd["h1T_s"] = (h1T_s, [B_h1T], [KC, P, T], BF16)
            A.release(mB)


        if stop not in ("p0", "pA", "pB"):
            mC = A.mark()
            wr = [A.alloc(f"wC{i}", KC * 512, BF16) for i in range(3)]
            creq = []
            for g_ in range(3):
                for half_ in range(2):
                    creq.append((kvw_d, 0 + g_ * 1024 + half_ * 512))
            for g_ in range(3):
                for half_ in range(2):
                    creq.append((kvw_d, 3072 + g_ * 1024 + half_ * 512))
            for g_ in range(3):
                for half_ in range(2):
                    creq.append((binw_d, 0 + g_ * 1024 + half_ * 512))
            cstate = [0]

            def issue_c(i):
                if i >= len(creq) or i < cstate[0]:
                    return
                cstate[0] += 1
                src_, c0_ = creq[i]
                wt_ = wr[i % 3]
                DMA("pool", wt_.ap.rearrange("p (k n) -> p k n", k=KC), src_.rearrange("(k p) n -> p k n", p=P)[:, :, c0_:c0_ + 512], w=[wt_])

            def get_w(c0_expect):
                nonlocal cw_i
                i = cw_i
                cw_i += 1
                assert creq[i][1] == c0_expect, (i, creq[i][1], c0_expect)
                for j_ in range(cstate[0], i + 3):
                    issue_c(j_)
                wt_ = wr[i % 3]
                return wt_, wt_.ap.rearrange("p (k n) -> p k n", k=KC)
            kto = [A.alloc(f"kto{i}", T, BF16) for i in range(2)]
            vo = [A.alloc(f"vo{i}", 512, BF16) for i in range(2)]
            psC = [PS.at(f"psC{i}", i * 512, 512) for i in range(4)]
            cw_i = 0
            ck_i = 0
            cv_i = 0
            cp_i = 0
            kvw3 = kvw_d.rearrange("(k p) n -> p k n", p=P)
            binw3 = binw_d.rearrange("(k p) n -> p k n", p=P)

            def featmajor_proj(src, src3, wsrc3, col0, dst_s, dst_b):
                nonlocal cw_i, ck_i, cp_i
                for g in range(3):
                    d = DIL[g][1]
                    for half in range(2):
                        c0 = col0 + g * 1024 + half * 512
                        wt, w3 = get_w(c0)
                        for hh in range(4):
                            h = half * 4 + hh
                            ko = kto[ck_i % 2]
                            ck_i += 1
                            ko3 = ko.ap.rearrange("p (r m) -> p r m", r=d)
                            for blk in range(T // 512):
                                ps = psC[cp_i % 4]
                                cp_i += 1
                                for k in range(KC):
                                    MM(ps, ps.ap, wt, w3[:, k, hh * P:(hh + 1) * P], src, src3[:, k, blk * 512:(blk + 1) * 512],
                                       k == 0, k == KC - 1)
                                mpb = 512 // d
                                oap = ko3[:, :, blk * mpb:(blk + 1) * mpb]
                                iap = ps.ap.rearrange("p (m r) -> p r m", r=d)
                                CP("act" if cp_i % 2 == 0 else "dve", ko, oap, ps, iap)
                            DMA("sp", dst_s[g * 8 + h], ko.ap, r=[ko], w=[DT(dst_b)], nowaw=True)

            xT = A.alloc("xT_C", KC * T, BF16)
            xT3 = xT.ap.rearrange("p (k t) -> p k t", k=KC)
            for k in range(KC):
                DMA("sp", xT3[:, k, :], x1T_s[k], r=[DT(B_x1T)], w=[xT], nowaw=True)
            featmajor_proj(xT, xT3, kvw3, 0, KT_s, B_KT)
            for g in range(3):
                d = DIL[g][1]
                for half in range(2):
                    c0 = 3072 + g * 1024 + half * 512
                    wt, w3 = get_w(c0)
                    for tt in range(T // P):
                        r_ = (tt * P) // (T // d)
                        mb = ((tt * P) % (T // d)) // P
                        base = mb * P * d + r_
                        ps = psC[cp_i % 4]
                        cp_i += 1
                        for k in range(KC):
                            MM(ps, ps.ap, xT, xT3[:, k, base:base + (P - 1) * d + 1:d], wt, w3[:, k, :], k == 0, k == KC - 1)
                        v_ = vo[cv_i % 2]
                        cv_i += 1
                        CP("act" if cv_i % 2 == 0 else "dve", v_, v_.ap, ps, ps.ap)
                        DMA("sp", V_s[g, tt * P:(tt + 1) * P, half * 512:(half + 1) * 512], v_.ap, r=[v_], w=[DT(B_V)], nowaw=True)
            for k in range(KC):
                DMA("sp", xT3[:, k, :], h1T_s[k], r=[DT(B_h1T)], w=[xT], nowaw=True)
            featmajor_proj(xT, xT3, binw3, 0, QT_s, B_QT)
            if stop == "pC":
                dbgd["KT_s"] = (KT_s, [B_KT], [24, P, T], BF16)
                dbgd["QT_s"] = (QT_s, [B_QT], [24, P, T], BF16)
                dbgd["V_s"] = (V_s.rearrange("g (a t) c -> (g a) t c", a=8), [B_V], [24, 512, 1024], BF16)
            A.release(mC)


        if stop not in ("p0", "pA", "pB", "pC"):
            mD = A.mark()
            SCALE = float(DS) ** -0.5
            ab = A.alloc("abias", 24 * 256)
            ab3 = ab.ap.rearrange("p (a k) -> p a k", a=24)
            DMA("sp", ab.ap, abias_d, w=[ab])
            negt = A.alloc("negt", P)
            MEMSET("pool", negt.buf, negt.ap, NEG)
            identb = A.alloc("identb", P, BF16)
            CP("pool", identb, identb.ap, cst, ident)
            NR = 3
            qt_r = [A.alloc(f"qt{i}", 8 * P, BF16) for i in range(NR)]
            kt_r = [A.alloc(f"kt{i}", 8 * 256, BF16) for i in range(NR)]
            v_r = [A.alloc(f"vt{i}", 2 * 1024, BF16) for i in range(NR)]
            o_r = [A.alloc(f"ot{i}", 1024) for i in range(NR)]
            l_r = [A.alloc(f"lt{i}", 8) for i in range(NR)]
            NHS = 6
            hsets = [dict(sbt=A.alloc(f"sbt{i}", 1024), pn=A.alloc(f"pn{i}", 1024, BF16), ptS=A.alloc(f"ptS{i}", 8 * P, BF16),
                          mx=A.alloc(f"mx{i}", 4), nmx=A.alloc(f"nmx{i}", 4), den=A.alloc(f"den{i}", 4),
                          rden=A.alloc(f"rden{i}", 4), lden=A.alloc(f"lden{i}", 4)) for i in range(NHS)]
            psSc = [PS.at(f"psSc{i}", i * 512, 512) for i in range(4)]
            psPT = [[PS.at(f"psPT{h}_{i}", 2048 + h * 512 + i * 256, 512, BF16) for i in range(2)] for h in range(2)]
            psOD = [PS.at(f"psOD{i}", 3072 + i * 512, 512) for i in range(2)]
            tiles = []
            for g in range(3):
                d = DIL[g][1]
                M = T // d
                for r_ in range(d):
                    for mb in range(M // P):
                        tiles.append((g, d, M, r_, mb))

            def gen_half(ti, half, hset):
                g, d, M, r_, mb = tiles[ti]
                Og = O_s[g].rearrange("(m r) c -> r m c", r=d)
                Lg = L_s[g].rearrange("(m r) c -> r m c", r=d)
                base = r_ * M + mb * P
                kb = base - P if mb > 0 else base
                qt = qt_r[ti % NR]
                kt = kt_r[ti % NR]
                vt = v_r[ti % NR]
                ot = o_r[ti % NR]
                lt = l_r[ti % NR]
                sbt, pn, ptS, mx, nmx, den, rden, lden = (hset[k_] for k_ in ("sbt", "pn", "ptS", "mx", "nmx", "den", "rden", "lden"))
                q3 = qt.ap.rearrange("p (h t) -> p h t", h=8)
                k3 = kt.ap.rearrange("p (h t) -> p h t", h=8)
                v3_ = vt.ap.rearrange("p (a c) -> p a c", a=2)
                if half == 0:
                    DMA("sp", q3, QT_s[g * 8:(g + 1) * 8, :, base:base + P].rearrange("h p t -> p h t"), r=[DT(B_QT)], w=[qt])
                    if mb > 0:
                        DMA("sp", k3, KT_s[g * 8:(g + 1) * 8, :, kb:kb + 256].rearrange("h p t -> p h t"), r=[DT(B_KT)], w=[kt])
                        DMA("sp", v3_, V_s[g, kb:kb + 256, :].rearrange("(a p) c -> p a c", a=2), r=[DT(B_V)], w=[vt])
                    else:
                        DMA("sp", k3[:, :, P:256], KT_s[g * 8:(g + 1) * 8, :, base:base + P].rearrange("h p t -> p h t"), r=[DT(B_KT)], w=[kt])
                        DMA("sp", v3_[:, 1, :], V_s[g, base:base + P, :], r=[DT(B_V)], w=[vt])
                    yield
                s3 = sbt.ap.rearrange("p (h k) -> p h k", h=4)
                for hh in range(4):
                    h = half * 4 + hh
                    ps = psSc[half * 2 + hh // 2]
                    pso = ps.ap[:, (hh % 2) * 256:(hh % 2 + 1) * 256]
                    if mb > 0:
                        MM(ps, pso, qt, q3[:, h, :], kt, k3[:, h, :], True, True)
                        STT(sbt, s3[:, hh, :], ps, pso, SCALE, ab, ab3[:, g * 8 + h, :], ALU.mult, ALU.add)
                    else:
                        MM(ps, pso[:, P:256], qt, q3[:, h, :], kt, k3[:, h, P:256], True, True)
                        STT(sbt, s3[:, hh, P:256], ps, pso[:, P:256], SCALE, ab, ab3[:, g * 8 + h, P:256], ALU.mult, ALU.add)
                    yield
                if mb == 0:
                    S.op("pool", lambda e, a=s3[:, :, 0:P]: e.memset(a, NEG), writes=[sbt.buf])
                S.op("dve", lambda e, o=mx.ap, i=s3: e.tensor_reduce(out=o, in_=i, axis=AX.X, op=ALU.max),
                     reads=[sbt.buf], writes=[mx.buf])
                yield
                TS("dve", nmx, nmx.ap, mx, mx.ap, -1.0, None, ALU.mult)
                yield
                for hh in range(4):
                    ACT(sbt, s3[:, hh, :], sbt, s3[:, hh, :], AF.Exp, bias=nmx.ap[:, hh:hh + 1], r=[nmx])
                    yield
                S.op("dve", lambda e, o=den.ap, i=s3: e.tensor_reduce(out=o, in_=i, axis=AX.X, op=ALU.add),
                     reads=[sbt.buf], writes=[den.buf])
                yield
                S.op("dve", lambda e, o=rden.ap, i=den.ap: e.reciprocal(out=o, in_=i), reads=[den.buf], writes=[rden.buf])
                ACT(lden, lden.ap, den, den.ap, AF.Ln)
                yield
                TT("dve", pn, pn.ap.rearrange("p (h k) -> p h k", h=4), sbt, s3,
                   rden, rden.ap.unsqueeze(2).to_broadcast([P, 4, 256]), ALU.mult)
                yield
                TT("dve", lt, lt.ap[:, half * 4:(half + 1) * 4], lden, lden.ap, mx, mx.ap, ALU.add)
                p3 = pn.ap.rearrange("p (h k) -> p h k", h=4)
                pts3 = ptS.ap.rearrange("p (a q) -> p a q", a=8)
                kts = (0, 1) if mb > 0 else (1,)
                pps = psPT[half]
                for hh in range(4):
                    for kt_i in kts:
                        pp = pps[hh // 2]
                        slot = (hh % 2) * 2 + kt_i
                        TR(pp, pp.ap[:, slot * P:(slot + 1) * P], pn, p3[:, hh, kt_i * P:(kt_i + 1) * P], identb, identb.ap)
                yield
                for i2 in range(2):
                    if mb > 0:
                        CP("act", ptS, ptS.ap[:, i2 * 512:(i2 + 1) * 512], pps[i2], pps[i2].ap)
                    else:
                        pv = pps[i2].ap.rearrange("p (a q) -> p a q", a=4)
                        CP("act", ptS, pts3[:, i2 * 4 + 1:i2 * 4 + 4:2, :], pps[i2], pv[:, 1:4:2, :])
                yield
                po = psOD[half]
                for hh in range(4):
                    h = half * 4 + hh
                    for kt_i in kts:
                        MM(po, po.ap[:, hh * P:(hh + 1) * P], ptS, pts3[:, hh * 2 + kt_i, :], vt, v3_[:, kt_i, h * P:(h + 1) * P],
                           kt_i == kts[0], kt_i == kts[-1])
                yield
                CP("act", ot, ot.ap[:, half * 512:(half + 1) * 512], po, po.ap)
                yield
                if half == 1:
                    DMA("act", Og[r_, mb * P:(mb + 1) * P, :], ot.ap, r=[ot], w=[DT(B_O)], nowaw=True)
                    DMA("act", Lg[r_, mb * P:(mb + 1) * P, :], lt.ap, r=[lt], w=[DT(B_L)], nowaw=True)

            work = [(ti, half) for ti in range(len(tiles)) for half in range(2)]
            active = []
            nxt_w = 0
            step = 0
            while nxt_w < len(work) or active:
                if nxt_w < len(work) and len(active) < D_DEPTH and (not active or step >= D_STAG):
                    ti, half = work[nxt_w]
                    active.append(gen_half(ti, half, hsets[nxt_w % NHS]))
                    nxt_w += 1
                    step = 0
                for g_ in list(active):
                    try:
                        next(g_)
                    except StopIteration:
                        active.remove(g_)
                step += 1
            if stop == "pD":
                dbgd["O_s"] = (O_s.rearrange("g (a t) c -> (g a) t c", a=8), [B_O], [24, 512, 1024], F32)
                dbgd["L_s"] = (L_s, [B_L], [3, T, 8], F32)
            A.release(mD)


        if stop not in ("p0", "pA", "pB", "pC", "pD"):
            mE = A.mark()
            g1b1 = A.alloc("g1b1", D)
            g1bc = [None, g1b1]
            DMA("sp", g1bc[1].ap, mod_s[0:1, 6144 + 4096:6144 + 6144].partition_broadcast(P), r=[DT(B_mod)], w=[g1bc[1]])
            TS("pool", g1bc[1], g1bc[1].ap, g1bc[1], g1bc[1].ap, 1.0, None, ALU.add)
            lng1 = A.alloc("lng1", D)
            lnb1 = A.alloc("lnb1", D)
            DMA("sp", lng1.ap, lng_d[1:2, :].partition_broadcast(P), w=[lng1])
            DMA("sp", lnb1.ap, lnb_d[1:2, :].partition_broadcast(P), w=[lnb1])
            wz = A.alloc("wz", KC * 1024, BF16)
            wz3 = wz.ap.rearrange("p (k n) -> p k n", k=KC)
            binw3e = binw_d.rearrange("(k p) n -> p k n", p=P)
            DMA("pool", wz3, binw3e[:, :, 3072:4096], w=[wz])
            bow = A.alloc("bow", 8 * D, BF16)
            bow3 = bow.ap.rearrange("p (i n) -> p i n", i=8)
            DMA("pool", bow3, bow_d.rearrange("(i p) n -> p i n", p=P), w=[bow])
            og_r = [[A.alloc(f"og{i}_{g}", 1024) for g in range(3)] for i in range(2)]
            lt_r = [A.alloc(f"ltE{i}", 24) for i in range(2)]
            h1_r = [A.alloc(f"h1E{i}", KC * P, BF16) for i in range(2)]
            x1_r = [A.alloc(f"x1E{i}", D) for i in range(2)]
            tE = [dict(szt=A.alloc(f"szt{i}", 1024), sg=[A.alloc(f"sgE{i}_{n}", 512) for n in range(2)], acc=A.alloc(f"accE{i}", 1024),
                       tmp=A.alloc(f"tmpE{i}", 1024), ogT=A.alloc(f"ogT{i}", 8 * P, BF16), rE=A.alloc(f"rE{i}", D),
                       lmx=A.alloc(f"lmx{i}", 8), lex=A.alloc(f"lex{i}", 24), lsm=A.alloc(f"lsm{i}", 8),
                       bst=A.alloc(f"bstE{i}", 24), mv=A.alloc(f"mvE{i}", 2), rs=A.alloc(f"rsE{i}", 1), nmr=A.alloc(f"nmrE{i}", 1))
                  for i in range(2)]
            psZ = [PS.at(f"psZ{i}", i * 512, 512) for i in range(2)]
            psTE = [PS.at(f"psTE{i}", 1024 + i * 512, 512) for i in range(2)]
            psOE = [PS.at(f"psOE{i}", 2048 + i * 512, 512) for i in range(4)]
            NT = T // P

            def rrE(*gens):
                gens = list(gens)
                while gens:
                    for g_ in list(gens):
                        try:
                            next(g_)
                        except StopIteration:
                            gens.remove(g_)

            def gen_E(j):
                ogs = og_r[j % 2]
                ltE = lt_r[j % 2]
                h1t = h1_r[j % 2]
                x1t = x1_r[j % 2]
                te = tE[j % 2]
                szt, acc, tmpE, ogT, rE = te["szt"], te["acc"], te["tmp"], te["ogT"], te["rE"]
                lmx, lex, lsm, bstE, mvE, rsE, nmrE = te["lmx"], te["lex"], te["lsm"], te["bst"], te["mv"], te["rs"], te["nmr"]
                l3 = ltE.ap.rearrange("p (g h) -> p g h", g=3)
                h13 = h1t.ap.rearrange("p (k t) -> p k t", k=KC)
                for g in range(3):
                    DMA("sp", ogs[g].ap, O_s[g, j * P:(j + 1) * P, :], r=[DT(B_O)], w=[ogs[g]])
                DMA("sp", l3, L_s[:, j * P:(j + 1) * P, :].rearrange("g p h -> p g h"), r=[DT(B_L)], w=[ltE])
                DMA("sp", h13, h1T_s[:, :, j * P:(j + 1) * P].rearrange("k p t -> p k t"), r=[DT(B_h1T)], w=[h1t])
                DMA("sp", x1t.ap, x1_s[j * P:(j + 1) * P, :], r=[DT(B_x1)], w=[x1t])
                yield
                TT("dve", lmx, lmx.ap, ltE, l3[:, 0, :], ltE, l3[:, 1, :], ALU.max)
                yield
                TT("dve", lmx, lmx.ap, lmx, lmx.ap, ltE, l3[:, 2, :], ALU.max)
                yield
                e3 = lex.ap.rearrange("p (g h) -> p g h", g=3)
                TT("dve", lex, e3, ltE, l3, lmx, lmx.ap.unsqueeze(1).to_broadcast([P, 3, 8]), ALU.subtract)
                yield
                ACT(lex, lex.ap, lex, lex.ap, AF.Exp)
                yield
                TT("dve", lsm, lsm.ap, lex, e3[:, 0, :], lex, e3[:, 1, :], ALU.add)
                yield
                TT("dve", lsm, lsm.ap, lsm, lsm.ap, lex, e3[:, 2, :], ALU.add)
                yield
                S.op("dve", lambda e, o=lsm.ap, i=lsm.ap: e.reciprocal(out=o, in_=i), reads=[lsm.buf], writes=[lsm.buf])
                yield
                TT("dve", lex, e3, lex, e3, lsm, lsm.ap.unsqueeze(1).to_broadcast([P, 3, 8]), ALU.mult)
                yield
                a3 = acc.ap.rearrange("p (h e) -> p h e", h=8)
                t3 = tmpE.ap.rearrange("p (h e) -> p h e", h=8)
                TT("dve", acc, a3, ogs[0], ogs[0].ap.rearrange("p (h e) -> p h e", h=8), lex, e3[:, 0, :].unsqueeze(2).to_broadcast([P, 8, P]), ALU.mult)
                yield
                for g in (1, 2):
                    TT("pool", tmpE, t3, ogs[g], ogs[g].ap.rearrange("p (h e) -> p h e", h=8), lex, e3[:, g, :].unsqueeze(2).to_broadcast([P, 8, P]), ALU.mult)
                    yield
                    TT("pool", acc, acc.ap, acc, acc.ap, tmpE, tmpE.ap, ALU.add)
                    yield
                for n in range(2):
                    sgE = te["sg"][n]
                    for k in range(KC):
                        MM(psZ[n], psZ[n].ap, h1t, h13[:, k, :], wz, wz3[:, k, n * 512:(n + 1) * 512], k == 0, k == KC - 1)
                    ACT(sgE, sgE.ap, psZ[n], psZ[n].ap, AF.Exp, scale=-1.0)
                    yield
                    ACT(sgE, sgE.ap, sgE, sgE.ap, AF.Ln, bias=1.0, scale=1.0)
                    yield
                    ACT(sgE, sgE.ap, sgE, sgE.ap, AF.Exp, scale=-1.0)
                    yield
                    TT("dve", szt, szt.ap[:, n * 512:(n + 1) * 512], psZ[n], psZ[n].ap, sgE, sgE.ap, ALU.mult)
                    yield
                TT("dve", acc, acc.ap, acc, acc.ap, szt, szt.ap, ALU.mult)
                yield
                ogT3 = ogT.ap.rearrange("p (i t) -> p i t", i=8)
                for q2 in range(2):
                    pt = psTE[q2]
                    for i in range(4):
                        TR(pt, pt.ap[:, i * P:(i + 1) * P], acc, acc.ap[:, (q2 * 4 + i) * P:(q2 * 4 + i + 1) * P], cst, ident)
                    CP("act", ogT, ogT.ap[:, q2 * 512:(q2 + 1) * 512], pt, pt.ap)
                    yield
                for n in range(4):
                    for i in range(8):
                        MM(psOE[n], psOE[n].ap, ogT, ogT3[:, i, :], bow, bow3[:, i, n * 512:(n + 1) * 512], i == 0, i == 7)
                    TT("dve", rE, rE.ap[:, n * 512:(n + 1) * 512], psOE[n], psOE[n].ap, g1bc[1], g1bc[1].ap[:, n * 512:(n + 1) * 512], ALU.mult)
                    yield
                STT(rE, rE.ap, x1t, x1t.ap, ALPHA, rE, rE.ap, ALU.mult, ALU.add)
                yield
                for n in range(4):
                    S.op("dve", lambda e, o=bstE.ap[:, n * 6:(n + 1) * 6], i=rE.ap[:, n * 512:(n + 1) * 512]: e.bn_stats(out=o, in_=i),
                         reads=[rE.buf], writes=[bstE.buf])
                yield
                S.op("dve", lambda e, o=mvE.ap, i=bstE.ap: e.bn_aggr(out=o, in_=i), reads=[bstE.buf], writes=[mvE.buf])
                yield
                ACT(rsE, rsE.ap, mvE, mvE.ap[:, 1:2], AF.Ln, bias=LN_EPS, scale=1.0)
                yield
                ACT(rsE, rsE.ap, rsE, rsE.ap, AF.Exp, scale=-0.5)
                yield
                TS("dve", nmrE, nmrE.ap, mvE, mvE.ap[:, 0:1], rsE.ap, -1.0, ALU.mult, ALU.mult, r=[rsE])
                yield
                ACT(rE, rE.ap, rE, rE.ap, AF.Identity, bias=nmrE.ap, scale=rsE.ap, r=[nmrE, rsE])
                yield
                TT("pool", rE, rE.ap, rE, rE.ap, lng1, lng1.ap, ALU.mult)
                yield
                TT("dve", rE, rE.ap, rE, rE.ap, lnb1, lnb1.ap, ALU.add)
                yield
                DMA("act", out_d[j * P:(j + 1) * P, :], rE.ap, r=[rE], w=[DT(B_out)], nowaw=True)

            def stag(j):
                g_ = gen_E(j)
                return g_
            active = []
            nxt_j = 0
            step = 0
            while nxt_j < NT or active:
                if nxt_j < NT and len(active) < E_DEPTH and (not active or step >= E_STAG):
                    active.append(gen_E(nxt_j))
                    nxt_j += 1
                    step = 0
                for g_ in list(active):
                    try:
                        next(g_)
                    except StopIteration:
                        active.remove(g_)
                step += 1
            A.release(mE)
            final_out = True

        A.release(base_mark)

        finals = [B_out] if stop in (None, "pE") else []
        for name, (t, shape) in dbg.items():
            od = nc.dram_tensor("dbg_" + name, list(shape), F32, kind="ExternalOutput").ap()
            ob = Buf("dbg_" + name)
            DMA("sp", od, t.ap, r=[t], w=[DT(ob)])
            finals.append(ob)
        for name, (src, bufs, shape, dt_) in dbgd.items():
            od = nc.dram_tensor("dbg_" + name, list(shape), dt_, kind="ExternalOutput").ap()
            ob = Buf("dbg_" + name)
            for i0 in range(shape[0]):
                S.op("sp", lambda e, o=od[i0], i=src[i0]: e.dma_start(out=o, in_=i), reads=list(bufs), writes=[ob], dma=True, nowaw=True)
            finals.append(ob)
        S.finish(finals)
        S.emit(st)
    print("sched stats", S.stats(), "sbuf peak words", A.peak, "n dma sems", len(S.dma_bufs))
    return nc, list(dbg.keys()) + list(dbgd.keys())


def phaseA(nc, S, A, PS, env):
    pass


def make_consts():
    cst = np.zeros((P, 384), np.float32)
    cst[:, 0:128] = np.eye(P, dtype=np.float32)
    i = np.arange(P)
    cst[:, 128:256] = (i[:, None] <= i[None, :]).astype(np.float32)
    cst[:, 256:384] = np.where(i[None, :] >= i[:, None], 0.0, NEG).astype(np.float32)
    n = 24
    slopes = (2.0 ** (-8.0 * np.arange(1, n + 1) / n)).reshape(3, 8)
    q = np.arange(128)[:, None]
    k = np.arange(256)[None, :]
    delta = q + 128 - k
    valid = (delta >= 0) & (delta <= 128)
    ab = np.zeros((P, 24, 256), np.float32)
    for g in range(3):
        for h in range(8):
            ab[:, g * 8 + h, :] = np.where(valid, -slopes[g, h] * delta * DIL[g][1], NEG)
    return cst, ab.reshape(P, 24 * 256)


def core_inputs(b, inputs, cst, ab):
    f = lambda a: np.ascontiguousarray(a, dtype=np.float32)
    cw = inputs["a_conv_w"][0]
    cwl = cw.reshape(4, 48, P).transpose(2, 1, 0).reshape(P, 48 * 4)
    cb = inputs["a_conv_b"][0].reshape(48, P).T
    Dc = np.repeat(inputs["a_D"][0], HD).reshape(32, P).T
    ng = inputs["a_norm_g"][0].reshape(32, P).T
    return {
        "x": f(inputs["x"][b]),
        "cT": f(inputs["c"][b].reshape(KC, P).T),
        "ada_w": f(inputs["ada_w"]),
        "ada_b": f(inputs["ada_b"].reshape(1, 12288)),
        "ln_g": f(inputs["ln_g"]),
        "ln_b": f(inputs["ln_b"]),
        "a_in_w": f(inputs["a_in_w"][0]),
        "cw": f(cwl),
        "cb": f(cb),
        "dtb": f(inputs["a_dt_bias"].reshape(1, NH)),
        "alog": f(inputs["a_A_log"].reshape(1, NH)),
        "Dc": f(Dc),
        "ng": f(ng),
        "a_out_w": f(inputs["a_out_w"][0]),
        "kv_w": f(inputs["kv_w"]),
        "b_in_w": f(inputs["b_in_w"][0]),
        "b_out_w": f(inputs["b_out_w"][0]),
        "cst": cst,
        "abias": ab,
    }


def kernel(**inputs):
    inputs = {k: np.asarray(v) for k, v in inputs.items()}
    cst, ab = make_consts()
    nc, _ = build()
    in_maps = [core_inputs(b, inputs, cst, ab) for b in range(NCORES)]
    res = run_bass_kernel_spmd(nc, in_maps, core_ids=list(range(NCORES)))
    out = np.stack([np.asarray(res.results[b]["out"]) for b in range(NCORES)], axis=0)
    return out.astype(np.float32)
```
